# Optimizing a Trainium2 kernel written in Bass

```python
import math
import jax, jax.numpy as jnp
from jax import lax
import numpy as np

D_MODEL = 2048
BATCH = 1
SEQ = 8192
DEPTH = 4

CHUNK = 64
N_MIXERS = 2
EXPAND = 2
D_INNER = EXPAND * D_MODEL

GLA_HEADS = 4
GLA_DK = (D_MODEL // 2) // GLA_HEADS
GLA_DV = D_INNER // GLA_HEADS
GLA_QK = GLA_HEADS * GLA_DK
GLA_GATE_RANK = 16
GLA_TAU = 16.0
GLA_IN_WIDTH = 2 * GLA_QK + 2 * D_INNER + GLA_GATE_RANK

S5_GROUP = 16
S5_STATE = 64
S5_GROUPS = D_INNER // S5_GROUP
S5_GROUPS_PER_BLOCK = 16
S5_BLOCKS = S5_GROUPS // S5_GROUPS_PER_BLOCK
S5_DT_MIN = 1e-3
S5_DT_MAX = 1e-1

N_GLA_LAYERS = (DEPTH + 1) // 2
N_S5_LAYERS = DEPTH // 2

DEEPNORM_ALPHA = (2 * DEPTH) ** 0.25
DEEPNORM_BETA = (8 * DEPTH) ** -0.25
LN_EPS = 1e-5
RMS_EPS = 1e-6

kernel_name = "hybrid_gla_s5_deepnorm_adaln"


def layer_norm(x, g, b):
    xf = x.astype(jnp.float32)
    mu = jnp.mean(xf, axis=-1, keepdims=True)
    var = jnp.mean(jnp.square(xf - mu), axis=-1, keepdims=True)
    y = (xf - mu) * lax.rsqrt(var + LN_EPS) * g.astype(jnp.float32) + b.astype(jnp.float32)
    return y.astype(x.dtype)


def gla_mixer(u, w_in, gate_w2, gate_b, norm_g, w_out):
    bsz, seq_len, _ = u.shape
    n_chunks = seq_len // CHUNK
    proj = u @ w_in
    q, k, v, z, g_lr = jnp.split(
        proj, [GLA_QK, 2 * GLA_QK, 2 * GLA_QK + D_INNER, 2 * GLA_QK + 2 * D_INNER], axis=-1)
    log_alpha = jax.nn.log_sigmoid((g_lr @ gate_w2 + gate_b).astype(jnp.float32)) / GLA_TAU

    def to_chunks(t, d):
        return t.astype(jnp.float32).reshape(bsz, n_chunks, CHUNK, GLA_HEADS, d).transpose(1, 0, 3, 2, 4)

    qc = to_chunks(q, GLA_DK) * (GLA_DK ** -0.5)
    kc = to_chunks(k, GLA_DK)
    vc = to_chunks(v, GLA_DV)
    gc = to_chunks(log_alpha, GLA_DK)

    def step(state, inp):
        q_c, k_c, v_c, g_c = inp
        cum = jnp.cumsum(g_c, axis=2)
        tot = cum[:, :, -1:, :]
        k_dec = k_c * jnp.exp(tot - cum)
        state = jnp.exp(tot[:, :, 0, :, None]) * state + jnp.einsum('bhck,bhcv->bhkv', k_dec, v_c)
        o = jnp.einsum('bhck,bhkv->bhcv', q_c, state)
        return state, o

    s0 = jnp.zeros((bsz, GLA_HEADS, GLA_DK, GLA_DV), jnp.float32)
    _, o = lax.scan(step, s0, (qc, kc, vc, gc))
    o = o.transpose(1, 0, 3, 2, 4).reshape(bsz, seq_len, GLA_HEADS, GLA_DV)
    o = o * lax.rsqrt(jnp.mean(jnp.square(o), axis=-1, keepdims=True) + RMS_EPS)
    o = o.reshape(bsz, seq_len, D_INNER) * norm_g.astype(jnp.float32)
    y = o.astype(u.dtype) * jax.nn.silu(z)
    return y @ w_out


def _linear_recurrence_combine(e1, e2):
    a1r, a1i, b1r, b1i = e1
    a2r, a2i, b2r, b2i = e2
    ar = a2r * a1r - a2i * a1i
    ai = a2r * a1i + a2i * a1r
    br = a2r * b1r - a2i * b1i + b2r
    bi = a2r * b1i + a2i * b1r + b2i
    return (ar, ai, br, bi)


def s5_mixer(u_in, w_in, a_re, a_im, log_dt, b_re, b_im, c_re, c_im, d_skip, w_glu, b_glu, w_out):
    bsz, seq_len, _ = u_in.shape
    proj = u_in @ w_in
    u, z = jnp.split(proj, 2, axis=-1)
    f32 = jnp.float32
    a_re = a_re.astype(f32); a_im = a_im.astype(f32)
    b_re = b_re.astype(f32); b_im = b_im.astype(f32)
    dt = jnp.exp(log_dt.astype(f32))[:, None]
    mag = jnp.exp(a_re * dt)
    lb_re = mag * jnp.cos(a_im * dt)
    lb_im = mag * jnp.sin(a_im * dt)
    nr = lb_re - 1.0
    den = jnp.square(a_re) + jnp.square(a_im)
    coef_re = (nr * a_re + lb_im * a_im) / den
    coef_im = (lb_im * a_re - nr * a_im) / den
    bb_re = coef_re[..., None] * b_re - coef_im[..., None] * b_im
    bb_im = coef_re[..., None] * b_im + coef_im[..., None] * b_re

    uf = u.astype(f32)
    u_blocks = uf.reshape(bsz, seq_len, S5_BLOCKS, S5_GROUPS_PER_BLOCK, S5_GROUP).transpose(2, 0, 1, 3, 4)

    def blk(args):
        u_b, lr_b, li_b, br_b, bi_b, cr_b, ci_b = args
        bu_r = jnp.einsum('blgi,gpi->blgp', u_b, br_b)
        bu_i = jnp.einsum('blgi,gpi->blgp', u_b, bi_b)
        a_r = jnp.broadcast_to(lr_b, bu_r.shape)
        a_i = jnp.broadcast_to(li_b, bu_i.shape)
        _, _, x_r, x_i = lax.associative_scan(_linear_recurrence_combine, (a_r, a_i, bu_r, bu_i), axis=1)
        return jnp.einsum('blgp,gip->blgi', x_r, cr_b) - jnp.einsum('blgp,gip->blgi', x_i, ci_b)

    nb, gpb = S5_BLOCKS, S5_GROUPS_PER_BLOCK
    y = lax.map(blk, (
        u_blocks,
        lb_re.reshape(nb, gpb, S5_STATE), lb_im.reshape(nb, gpb, S5_STATE),
        bb_re.reshape(nb, gpb, S5_STATE, S5_GROUP), bb_im.reshape(nb, gpb, S5_STATE, S5_GROUP),
        c_re.astype(f32).reshape(nb, gpb, S5_GROUP, S5_STATE),
        c_im.astype(f32).reshape(nb, gpb, S5_GROUP, S5_STATE)))
    y = y.transpose(1, 2, 0, 3, 4).reshape(bsz, seq_len, D_INNER) + d_skip.astype(f32) * uf
    y = jax.nn.gelu(y)
    y = y * jax.nn.sigmoid(y @ w_glu.astype(f32) + b_glu.astype(f32))
    y = y.astype(u_in.dtype) * jax.nn.silu(z)
    return y @ w_out


def setup_inputs(seed: int = 0) -> dict:
    key = jax.random.key(seed)
    ks = jax.random.split(key, 24)
    nrm = jax.random.normal
    d, e = D_MODEL, D_INNER
    x = nrm(ks[0], (BATCH, SEQ, d), jnp.float32)
    c = nrm(ks[1], (BATCH, d), jnp.float32)
    ln_g = 1.0 + 0.01 * nrm(ks[2], (DEPTH, d), jnp.float32)
    ln_b = 0.01 * nrm(ks[3], (DEPTH, d), jnp.float32)
    ada_w = 0.2 * d ** -0.5 * nrm(ks[4], (DEPTH, d, 3 * d), jnp.float32)
    ada_b = 0.01 * nrm(ks[5], (DEPTH, 3 * d), jnp.float32)
    gla_w_in = d ** -0.5 * nrm(ks[6], (N_GLA_LAYERS, d, GLA_IN_WIDTH), jnp.float32)
    gla_gate_w2 = GLA_GATE_RANK ** -0.5 * nrm(ks[7], (N_GLA_LAYERS, GLA_GATE_RANK, GLA_QK), jnp.float32)
    gla_gate_b = 0.01 * nrm(ks[8], (N_GLA_LAYERS, GLA_QK), jnp.float32)
    gla_norm_g = 1.0 + 0.01 * nrm(ks[9], (N_GLA_LAYERS, e), jnp.float32)
    gla_w_out = DEEPNORM_BETA * e ** -0.5 * nrm(ks[10], (N_GLA_LAYERS, e, d), jnp.float32)
    s5_w_in = d ** -0.5 * nrm(ks[11], (N_S5_LAYERS, d, 2 * e), jnp.float32)
    s5_a_re = -0.5 + 0.01 * nrm(ks[12], (N_S5_LAYERS, S5_GROUPS, S5_STATE), jnp.float32)
    s5_a_im = jnp.broadcast_to(jnp.pi * jnp.arange(S5_STATE, dtype=jnp.float32),
                               (N_S5_LAYERS, S5_GROUPS, S5_STATE))
    s5_log_dt = jax.random.uniform(ks[13], (N_S5_LAYERS, S5_GROUPS), jnp.float32,
                                   math.log(S5_DT_MIN), math.log(S5_DT_MAX))
    bscale = (2 * S5_GROUP) ** -0.5
    s5_b_re = bscale * nrm(ks[14], (N_S5_LAYERS, S5_GROUPS, S5_STATE, S5_GROUP), jnp.float32)
    s5_b_im = bscale * nrm(ks[15], (N_S5_LAYERS, S5_GROUPS, S5_STATE, S5_GROUP), jnp.float32)
    cscale = S5_STATE ** -0.5
    s5_c_re = cscale * nrm(ks[16], (N_S5_LAYERS, S5_GROUPS, S5_GROUP, S5_STATE), jnp.float32)
    s5_c_im = cscale * nrm(ks[17], (N_S5_LAYERS, S5_GROUPS, S5_GROUP, S5_STATE), jnp.float32)
    s5_d = nrm(ks[18], (N_S5_LAYERS, e), jnp.float32)
    s5_w_glu = e ** -0.5 * nrm(ks[19], (N_S5_LAYERS, e, e), jnp.float32)
    s5_b_glu = 0.01 * nrm(ks[20], (N_S5_LAYERS, e), jnp.float32)
    s5_w_out = DEEPNORM_BETA * e ** -0.5 * nrm(ks[21], (N_S5_LAYERS, e, d), jnp.float32)
    return {"x": x, "c": c, "ln_g": ln_g, "ln_b": ln_b, "ada_w": ada_w, "ada_b": ada_b,
            "gla_w_in": gla_w_in, "gla_gate_w2": gla_gate_w2, "gla_gate_b": gla_gate_b,
            "gla_norm_g": gla_norm_g, "gla_w_out": gla_w_out,
            "s5_w_in": s5_w_in, "s5_a_re": s5_a_re, "s5_a_im": s5_a_im, "s5_log_dt": s5_log_dt,
            "s5_b_re": s5_b_re, "s5_b_im": s5_b_im, "s5_c_re": s5_c_re, "s5_c_im": s5_c_im,
            "s5_d": s5_d, "s5_w_glu": s5_w_glu, "s5_b_glu": s5_b_glu, "s5_w_out": s5_w_out}


def reference(x, c, ln_g, ln_b, ada_w, ada_b, gla_w_in, gla_gate_w2, gla_gate_b, gla_norm_g, gla_w_out,
              s5_w_in, s5_a_re, s5_a_im, s5_log_dt, s5_b_re, s5_b_im, s5_c_re, s5_c_im, s5_d,
              s5_w_glu, s5_b_glu, s5_w_out):
    mod = jnp.einsum('bd,lde->lbe', jax.nn.silu(c), ada_w) + ada_b[:, None, :]
    for i in range(DEPTH):
        shift, scale, gate = jnp.split(mod[i], 3, axis=-1)
        u = x * (1.0 + scale[:, None, :]) + shift[:, None, :]
        j = i // N_MIXERS
        if i % N_MIXERS == 0:
            h = gla_mixer(u, gla_w_in[j], gla_gate_w2[j], gla_gate_b[j], gla_norm_g[j], gla_w_out[j])
        else:
            h = s5_mixer(u, s5_w_in[j], s5_a_re[j], s5_a_im[j], s5_log_dt[j], s5_b_re[j], s5_b_im[j],
                         s5_c_re[j], s5_c_im[j], s5_d[j], s5_w_glu[j], s5_b_glu[j], s5_w_out[j])
        x = layer_norm(DEEPNORM_ALPHA * x + (1.0 + gate[:, None, :]) * h, ln_g[i], ln_b[i])
    return x
```

```python
import numpy as np
from contextlib import ExitStack
import concourse.bass as bass
import concourse.mybir as mybir
from concourse.bass_utils import run_bass_kernel_spmd

F32 = mybir.dt.float32
BF16 = mybir.dt.bfloat16
AF = mybir.ActivationFunctionType
ALU = mybir.AluOpType
AX = mybir.AxisListType

NCORES = 8
T = 1024
NT = 8
D = 2048
DC = 16
E = 4096
EC = 32
DEPTH = 4
GLA_IN = 10256
ALPHA = float((2 * DEPTH) ** 0.25)
LN_EPS = 1e-5
RMS_EPS = 1e-6
TAU = 16.0

N_LAYERS = DEPTH
PHASE_LIMIT = None
SUB_LIMIT = None
UT_DVE_ONLY = True
USE_SWDGE = False


class WLoader:
    def __init__(self, g, es, name, nstage, stage_elems):
        self.S = g.S
        self.n = 0
        self.stg = []
        if not USE_SWDGE:
            self.stg = [sb(es, g.nc, "%s_wstg%d" % (name, i), [128, stage_elems], F32) for i in range(nstage)]

    def load(self, dst, src, kc, ncols, wkey):
        S = self.S
        if USE_SWDGE:
            S.op("pool", lambda e: e.dma_start(out=dst, in_=src), w=[wkey], slot=("wl",) + tuple(wkey))
            return
        i = self.n % len(self.stg)
        self.n += 1
        st = self.stg[i][:, 0:kc * ncols].rearrange("p (a b) -> p a b", b=ncols)
        S.op("sp", lambda e: e.dma_start(out=st, in_=src), w=[("wstg", i)], slot=("wstg", i))
        S.op("pool", lambda e: e.tensor_copy(out=dst, in_=st), r=[("wstg", i)], w=[wkey])


class _StopBuild(Exception):
    pass
ENGS = ("pe", "act", "dve", "pool", "sp")


class _Op:
    __slots__ = ("eng", "fn", "slot", "sig", "deps", "sem", "val", "inc")


class Sched:
    def __init__(self, nc, es):
        self.nc = nc
        self.es = es
        self.ops = []
        self.lastw = {}
        self.readers = {}
        self.esem = {e: es.enter_context(nc.semaphore("sem_" + e)) for e in ENGS}
        self.ecount = {e: 0 for e in ENGS}
        self.waited = {e: {} for e in ENGS}
        self.dsem = {}
        self.phys = {e: [] for e in ENGS}
        self.nphase = 0

    def _slot(self, slot):
        if slot not in self.dsem:
            name = "ds_" + "_".join(str(s) for s in (slot if isinstance(slot, tuple) else (slot,)))
            self.dsem[slot] = [self.es.enter_context(self.nc.semaphore(name)), 0]
        return self.dsem[slot]

    def op(self, eng, fn, r=(), w=(), slot=None, inc=None):
        o = _Op()
        o.eng, o.fn, o.slot = eng, fn, slot
        o.sig = slot is not None
        o.inc = inc if inc is not None else (16 if slot is not None else 1)
        o.sem = None
        o.val = 0
        deps = {}
        for k in tuple(r) + tuple(w):
            d = self.lastw.get(k)
            if d is not None:
                deps[id(d)] = d
        for k in w:
            rd = self.readers.get(k)
            if rd:
                for d in rd.values():
                    if isinstance(d, list):
                        for x in d:
                            deps[id(x)] = x
                    else:
                        deps[id(d)] = d
        for k in w:
            self.lastw[k] = o
            self.readers[k] = {}
        for k in r:
            rd = self.readers.setdefault(k, {})
            if slot is not None:
                rd.setdefault("dma", []).append(o)
            else:
                rd[eng] = o
        deps.pop(id(o), None)
        o.deps = list(deps.values())
        self.ops.append(o)
        return o

    def flush(self):
        nc = self.nc
        ops = self.ops
        per = {e: [o for o in ops if o.eng == e] for e in ENGS}
        for o in ops:
            for d in o.deps:
                if d.slot is None:
                    if d.eng == o.eng == "pe":
                        continue
                    d.sig = True
        for e in ENGS:
            if per[e]:
                last = per[e][-1]
                if last.slot is None:
                    last.sig = True
        slotmap = {}
        nused = {e: 0 for e in ENGS}
        for e in ENGS:
            for o in per[e]:
                if o.slot is not None:
                    if o.inc == 1:
                        s = self._slot((e, o.slot))
                    else:
                        key = (e, o.slot)
                        if key not in slotmap:
                            if nused[e] >= len(self.phys[e]):
                                nm = "dq_%s_%d" % (e, len(self.phys[e]))
                                self.phys[e].append([self.es.enter_context(self.nc.semaphore(nm)), 0])
                            slotmap[key] = self.phys[e][nused[e]]
                            nused[e] += 1
                        s = slotmap[key]
                    s[1] += o.inc
                    o.sem, o.val = s[0], s[1]
                elif o.sig:
                    self.ecount[e] += 1
                    o.sem, o.val = self.esem[e], self.ecount[e]
        finals = [(self.esem[e], self.ecount[e]) for e in ENGS if self.ecount[e] > 0]
        finals += [(s[0], s[1]) for s in self.dsem.values() if s[1] > 0]
        finals += [(s[0], s[1]) for e in ENGS for s in self.phys[e] if s[1] > 0]

        def run(e, eng):
            wd = self.waited[e]
            for o in per[e]:
                need = {}
                for d in o.deps:
                    if d.slot is None and d.eng == e == "pe":
                        continue
                    k = id(d.sem)
                    if k not in need or need[k][1] < d.val:
                        need[k] = (d.sem, d.val)
                for k, (sem, val) in need.items():
                    if wd.get(k, 0) >= val:
                        continue
                    eng.wait_ge(sem, val)
                    wd[k] = val
                inst = o.fn(eng)
                if o.sig:
                    inst.then_inc(o.sem, o.inc)
            for sem, val in finals:
                k = id(sem)
                if sem is self.esem[e] and e in ("sp",):
                    pass
                if wd.get(k, 0) >= val:
                    continue
                if sem is self.esem[e] and e == "pe":
                    continue
                eng.wait_ge(sem, val)
                wd[k] = val

        with nc.Block() as block:
            @block.sync
            def _(eng):
                run("sp", eng)

            @block.gpsimd
            def _(eng):
                run("pool", eng)

            @block.scalar
            def _(eng):
                run("act", eng)

            @block.vector
            def _(eng):
                run("dve", eng)

            @block.tensor
            def _(eng):
                run("pe", eng)
        self.ops = []
        self.lastw = {}
        self.readers = {}
        self.nphase += 1
        if PHASE_LIMIT is not None and self.nphase >= PHASE_LIMIT:
            raise _StopBuild()


class Ctx:
    pass


def _consts_np():
    ident = np.eye(128, dtype=np.float32)
    s = np.arange(128)[:, None]
    t = np.arange(128)[None, :]
    triu = ((s > t) & ((s // 64) == (t // 64))).astype(np.float32)
    chunkind = np.zeros((128, 2), np.float32)
    chunkind[:64, 0] = 1.0
    chunkind[64:, 1] = 1.0
    ones = np.ones((128, 128), np.float32)
    return np.concatenate([ident, triu, ones, chunkind], axis=1)


def build_program(n_layers=None):
    n_layers = N_LAYERS if n_layers is None else n_layers
    nc = bass.Bass("TRN2", target_bir_lowering=False)
    g = Ctx()
    g.nc = nc

    def din(name, shape, dt=F32):
        return nc.dram_tensor(name, list(shape), dt, kind="ExternalInput").ap()

    def dint(name, shape, dt=F32):
        return nc.dram_tensor(name, list(shape), dt)

    g.x_in = din("x_in", [T, D])
    g.y_out = nc.dram_tensor("y_out", [T, D], F32, kind="ExternalOutput").ap()
    g.cT = din("cT", [128, DC])
    g.adaw = din("adaw", [DEPTH, D, 768])
    g.adab = din("adab", [128, DEPTH * 6])
    g.consts = din("consts", [128, 386])
    g.rmask = din("rmask", [128, 8])
    g.ln_g = din("ln_g", [DEPTH, D])
    g.ln_b = din("ln_b", [DEPTH, D])
    g.gla_w_in = din("gla_w_in", [2, D, GLA_IN])
    g.gla_gw2 = din("gla_gw2", [2, 16, 1024])
    g.gla_gb = din("gla_gb", [2, 1024])
    g.gla_ng = din("gla_ng", [2, 128, EC])
    g.gla_w_out = din("gla_w_out", [2, E, D])
    if n_layers < 2:
        din = lambda name, shape, dt=F32: None
    g.consts2 = din("consts2", [128, 264])
    g.s5_w_in = din("s5_w_in", [2, D, 2 * E])
    g.s5_w_glu = din("s5_w_glu", [2, E, E])
    g.s5_w_out = din("s5_w_out", [2, E, D])
    g.s5_are2 = din("s5_are2", [2, 128, 256])
    g.s5_aim2 = din("s5_aim2", [2, 128, 256])
    g.s5_ldt2 = din("s5_ldt2", [2, 128, 256])
    g.s5_bre2 = din("s5_bre2", [2, 128, 256, 16])
    g.s5_bim2 = din("s5_bim2", [2, 128, 256, 16])
    g.s5_cA = din("s5_cA", [2, 128, 256, 16])
    g.s5_cB = din("s5_cB", [2, 128, 256, 16])
    g.s5_dtab = din("s5_dtab", [2, 128, 256, 16])
    g.s5_bglu = din("s5_bglu", [2, 128, EC])

    g.xbuf = [dint("xbuf0", [T, D]).ap(), dint("xbuf1", [T, D]).ap()]
    g.mod_b = dint("mod_b", [128, 24])
    g.mod_a = dint("mod_a", [NCORES * 128, 24])
    g.modfm = dint("modfm", [128, DEPTH * 48]).ap()
    g.kd_d = dint("kd_d", [NT, 128, 1024], BF16).ap()
    g.v_d = dint("v_d", [NT, 128, E], BF16).ap()
    g.qT_d = dint("qT_d", [8, 128, T], BF16).ap()
    g.zg_d = dint("zg_d", [EC, 128, T], BF16).ap()
    g.yT_d = dint("yT_d", [EC, 128, T], BF16).ap()
    g.dec_d = dint("dec_d", [128, 128]).ap()
    g.tb_d = dint("tb_d", [128, 8])
    g.ta_d = dint("ta_d", [NCORES * 128, 8])
    g.Lb_d = dint("Lb_d", [1024, 1024])
    g.La_d = dint("La_d", [NCORES * 1024, 1024])
    g.rot_d = dint("rot_d", [128, 4, 256]).ap()
    g.MT_d = dint("MT_d", [128, 256, 128], BF16).ap()
    g.BqT_d = dint("BqT_d", [128, 256, 128], BF16).ap()
    g.CmK_d = dint("CmK_d", [128, 256, 128], BF16).ap()
    g.Uk_d = dint("Uk_d", [128, 256 * 128], BF16).ap()
    g.Ust_d = dint("Ust_d", [128, 256, 128], BF16).ap()
    g.W_d = dint("W_d", [128, 256, 128]).ap()
    g.ZP_d = dint("ZP_d", [128, 256, 128], BF16).ap()
    g.zb_d = dint("zb_d", [128, 256])
    g.za_d = dint("za_d", [NCORES * 128, 256])
    g.ygT_d = dint("ygT_d", [EC, 128, T], BF16).ap()

    with ExitStack() as es:
        S = Sched(nc, es)
        g.S = S
        try:
            _build_all(g, n_layers)
        except _StopBuild:
            pass
    return nc


def _build_all(g, n_layers):
    if True:
        phase_mod(g)
        for l in range(n_layers):
            last = (l == n_layers - 1)
            xin = g.x_in if l == 0 else g.xbuf[(l - 1) % 2]
            xout = g.y_out if last else g.xbuf[l % 2]
            if l % 2 == 0:
                gla_layer(g, l, xin, xout)
            else:
                s5_layer(g, l, xin, xout)


_UID = [0]


def _uname(name):
    _UID[0] += 1
    return "%s_u%d" % (name, _UID[0])


def sb(es, nc, name, shape, dt):
    return es.enter_context(nc.sbuf_tensor(_uname(name), list(shape), dt))


def ps(es, nc, name, shape, dt=F32):
    return es.enter_context(nc.psum_tensor(_uname(name), list(shape), dt))


def phase_mod(g):
    nc, S = g.nc, g.S
    with ExitStack() as es:
        cT = sb(es, nc, "m_cT", [128, DC], F32)
        sc = sb(es, nc, "m_sc", [128, DC], F32)
        wsl = [sb(es, nc, "m_w%d" % i, [128, DC, 768], F32) for i in range(2)]
        bia = sb(es, nc, "m_b", [128, 24], F32)
        res = sb(es, nc, "m_res", [128, 24], F32)
        pm = ps(es, nc, "m_pm", [128, 512])
        S.op("sp", lambda e: e.dma_start(out=cT[:], in_=g.cT), w=["cT"], slot="ld0")
        S.op("sp", lambda e: e.dma_start(out=bia[:], in_=g.adab), w=["bia"], slot="ld1")
        S.op("act", lambda e: e.activation(out=sc[:], in_=cT[:], func=AF.Silu), r=["cT"], w=["sc"])
        for l in range(DEPTH):
            s = l % 2
            for q in range(2):
                S.op("sp", lambda e, l=l, s=s, q=q: e.dma_start(
                    out=wsl[s][:, q * 8:(q + 1) * 8, :],
                    in_=g.adaw[l, q * 1024:(q + 1) * 1024, :].rearrange("(dc p) n -> p dc n", p=128)),
                    w=[("w", s, q)], slot=("mw", s, q))

            def mm(e, l=l, s=s):
                inst = None
                for j in range(6):
                    for dc in range(DC):
                        inst = e.matmul(pm[:, l * 6 + j:l * 6 + j + 1], wsl[s][:, dc, j * 128:(j + 1) * 128],
                                        sc[:, dc:dc + 1], start=(dc == 0), stop=(dc == DC - 1))
                return inst
            S.op("pe", mm, r=[("w", s, 0), ("w", s, 1), "sc"], w=[("pm", l)])
        S.op("dve", lambda e: e.tensor_tensor(out=res[:], in0=pm[:, 0:24], in1=bia[:], op=ALU.add),
             r=[("pm", l) for l in range(DEPTH)] + ["bia"], w=["res"])
        S.op("sp", lambda e: e.dma_start(out=g.mod_b.ap(), in_=res[:]), r=["res"], w=["mod_b"], slot="st0")
        S.op("pool", lambda e: e.collective_compute(
            "AllGather", ALU.bypass, replica_groups=[list(range(NCORES))],
            ins=[g.mod_b.ap().opt()], outs=[g.mod_a.ap().opt()]),
            r=["mod_b"], w=["mod_a"], slot="cc_mod", inc=1)
        alls = sb(es, nc, "m_all", [128, NCORES, 24], F32)
        fm = sb(es, nc, "m_fm", [128, DEPTH, 48], F32)
        S.op("sp", lambda e: e.dma_start(out=alls[:], in_=g.mod_a.ap().rearrange("(r p) f -> p r f", p=128)),
             r=["mod_a"], w=["alls"], slot="ld0")
        for l in range(DEPTH):
            S.op("dve", lambda e, l=l: e.tensor_copy(
                out=fm[:, l, :].rearrange("p (r j) -> p r j", j=6), in_=alls[:, :, l * 6:(l + 1) * 6]),
                r=["alls"], w=[("fm", l)])
            S.op("dve", lambda e, l=l: e.tensor_scalar(out=fm[:, l, 16:48], in0=fm[:, l, 16:48], scalar1=1.0,
                                                      scalar2=None, op0=ALU.add),
                 r=[("fm", l)], w=[("fm", l)])
        S.op("sp", lambda e: e.dma_start(out=g.modfm, in_=fm[:].rearrange("p l c -> p (l c)")),
             r=[("fm", l) for l in range(DEPTH)], w=["modfm"], slot="st0")
        S.flush()


def emit_build_uT(g, es, l, xin, uT, cst, pa, modv):
    nc, S = g.nc, g.S
    xt = [sb(es, nc, "u_xt%d" % i, [128, D], F32) for i in range(2)]
    ident = cst[:, 0:128]
    for i in range(NT):
        s = i % 2
        S.op("sp", lambda e, i=i, s=s: e.dma_start(out=xt[s][:], in_=xin[i * 128:(i + 1) * 128, :]),
             w=[("xt", s)], slot=("xt", s))
        for b in range(4):
            def tr(e, s=s, b=b):
                inst = None
                for q in range(4):
                    dc = b * 4 + q
                    inst = e.transpose(pa[b][:, q * 128:(q + 1) * 128], xt[s][:, dc * 128:(dc + 1) * 128], ident)
                return inst
            S.op("pe", tr, r=[("xt", s), "cst"], w=[("pa", b)])
            for q in range(4):
                dc = b * 4 + q
                if dc % 2 == 0 or UT_DVE_ONLY:
                    S.op("dve", lambda e, i=i, b=b, q=q, dc=dc: e.tensor_scalar(
                        out=uT[:, dc, i * 128:(i + 1) * 128], in0=pa[b][:, q * 128:(q + 1) * 128],
                        scalar1=modv[:, 16 + dc:17 + dc], scalar2=modv[:, dc:dc + 1], op0=ALU.mult, op1=ALU.add),
                        r=[("pa", b), "modv"], w=[("uT", i, dc)])
                else:
                    S.op("act", lambda e, i=i, b=b, q=q, dc=dc: e.activation(
                        out=uT[:, dc, i * 128:(i + 1) * 128], in_=pa[b][:, q * 128:(q + 1) * 128],
                        func=AF.Identity, bias=modv[:, dc:dc + 1], scale=modv[:, 16 + dc:17 + dc]),
                        r=[("pa", b), "modv"], w=[("uT", i, dc)])


def emit_outproj_ln(g, l, w_out, yT_d, xin, xout):
    nc, S = g.nc, g.S
    with ExitStack() as es:
        yT = sb(es, nc, "o_yT", [128, EC, T], BF16)
        wo = [sb(es, nc, "o_wo%d" % i, [128, EC, 256], BF16) for i in range(2)]
        hb = sb(es, nc, "o_hb", [128, NT, D], F32)
        grow = sb(es, nc, "o_grow", [128, D], F32)
        lng = sb(es, nc, "o_lng", [128, D], F32)
        lnb = sb(es, nc, "o_lnb", [128, D], F32)
        xt = sb(es, nc, "o_xt", [128, D], F32)
        cst = sb(es, nc, "o_cst", [128, 386], F32)
        modv = sb(es, nc, "o_modv", [128, 48], F32)
        gb = sb(es, nc, "o_gb", [128, 128], F32)
        st = sb(es, nc, "o_st", [128, 4, 6], F32)
        mv = sb(es, nc, "o_mv", [128, 2], F32)
        rs = sb(es, nc, "o_rs", [128, 1], F32)
        pa = [ps(es, nc, "o_pa%d" % i, [128, 512]) for i in range(8)]
        ident = cst[:, 0:128]
        wl = WLoader(g, es, "o", 1, 8 * 256)
        S.op("sp", lambda e: e.dma_start(out=cst[:], in_=g.consts), w=["cst"], slot="ld0")
        S.op("sp", lambda e: e.dma_start(out=modv[:], in_=g.modfm[:, l * 48:(l + 1) * 48]), w=["modv"], slot="ld1")
        S.op("sp", lambda e: e.dma_start(out=lng[:], in_=g.ln_g[l:l + 1, :].broadcast_to([128, D])),
             w=["lng"], slot="ld2")
        S.op("sp", lambda e: e.dma_start(out=lnb[:], in_=g.ln_b[l:l + 1, :].broadcast_to([128, D])),
             w=["lnb"], slot="ld3")
        for q in range(4):
            S.op("sp", lambda e, q=q: e.dma_start(
                out=yT[:, q * 8:(q + 1) * 8, :], in_=yT_d[q * 8:(q + 1) * 8].rearrange("f p t -> p f t")),
                w=[("yT", q)], slot=("yT", q))
        for dc in range(DC):
            b = dc // 4
            S.op("dve", lambda e, dc=dc: e.tensor_copy(out=gb[:], in_=modv[:, 32 + dc:33 + dc].broadcast_to([128, 128])),
                 r=["modv"], w=["gb"])
            S.op("pe", lambda e, dc=dc, b=b: e.matmul(pa[b][:, (dc % 4) * 128:(dc % 4 + 1) * 128], gb[:], ident,
                                                     start=True, stop=True),
                 r=["gb", "cst"], w=[("pa", b)])
        for b in range(4):
            S.op("act", lambda e, b=b: e.activation(out=grow[:, b * 512:(b + 1) * 512], in_=pa[b][:], func=AF.Identity),
                 r=[("pa", b)], w=[("grow", b)])
        k = 0
        for cg in range(8):
            s = cg % 2
            for q in range(2):
                for q2 in range(2):
                    r0 = q * 2048 + q2 * 1024
                    wl.load(wo[s][:, q * 16 + q2 * 8:q * 16 + q2 * 8 + 8, :],
                            w_out[r0:r0 + 1024, cg * 256:(cg + 1) * 256].rearrange("(kc p) n -> p kc n", p=128),
                            8, 256, ("wo", s, q, q2))
            for i in range(NT):
                b = k % 8
                k += 1

                def mm(e, i=i, s=s, b=b):
                    inst = None
                    for kc in range(EC):
                        inst = e.matmul(pa[b][:, 0:256], yT[:, kc, i * 128:(i + 1) * 128], wo[s][:, kc, :],
                                        start=(kc == 0), stop=(kc == EC - 1))
                    return inst
                S.op("pe", mm, r=[("yT", q) for q in range(4)] + [("wo", s, a_, b_) for a_ in range(2) for b_ in range(2)],
                     w=[("pa", b)])
                S.op("dve", lambda e, i=i, cg=cg, b=b: e.tensor_tensor(
                    out=hb[:, i, cg * 256:(cg + 1) * 256], in0=pa[b][:, 0:256], in1=grow[:, cg * 256:(cg + 1) * 256],
                    op=ALU.mult), r=[("pa", b)] + [("grow", q) for q in range(4)], w=[("hb", i, cg)])
        for i in range(NT):
            S.op("sp", lambda e, i=i: e.dma_start(out=xt[:], in_=xin[i * 128:(i + 1) * 128, :]), w=["xt"], slot="xt0")
            S.op("dve", lambda e, i=i: e.scalar_tensor_tensor(out=hb[:, i, :], in0=xt[:], scalar=ALPHA, in1=hb[:, i, :],
                                                             op0=ALU.mult, op1=ALU.add),
                 r=["xt"] + [("hb", i, cg) for cg in range(8)], w=[("hb", i)])
            for q in range(4):
                S.op("dve", lambda e, i=i, q=q: e.bn_stats(out=st[:, q, :], in_=hb[:, i, q * 512:(q + 1) * 512]),
                     r=[("hb", i)], w=[("st", q)])
            S.op("dve", lambda e: e.bn_aggr(out=mv[:], in_=st[:].rearrange("p a b -> p (a b)")),
                 r=[("st", q) for q in range(4)], w=["mv"])
            S.op("dve", lambda e: e.tensor_scalar(out=rs[:], in0=mv[:, 1:2], scalar1=LN_EPS, scalar2=None, op0=ALU.add),
                 r=["mv"], w=["rs"])
            S.op("act", lambda e: e.activation(out=rs[:], in_=rs[:], func=AF.Sqrt), r=["rs"], w=["rs"])
            S.op("dve", lambda e: e.reciprocal(out=rs[:], in_=rs[:]), r=["rs"], w=["rs"])
            S.op("dve", lambda e, i=i: e.tensor_scalar(out=hb[:, i, :], in0=hb[:, i, :], scalar1=mv[:, 0:1],
                                                      scalar2=rs[:, 0:1], op0=ALU.subtract, op1=ALU.mult),
                 r=[("hb", i), "mv", "rs"], w=[("hb", i)])
            S.op("dve", lambda e, i=i: e.tensor_tensor(out=hb[:, i, :], in0=hb[:, i, :], in1=lng[:], op=ALU.mult),
                 r=[("hb", i), "lng"], w=[("hb", i)])
            S.op("dve", lambda e, i=i: e.tensor_tensor(out=hb[:, i, :], in0=hb[:, i, :], in1=lnb[:], op=ALU.add),
                 r=[("hb", i), "lnb"], w=[("hb", i)])
            S.op("sp", lambda e, i=i: e.dma_start(out=xout[i * 128:(i + 1) * 128, :], in_=hb[:, i, :]),
                 r=[("hb", i)], w=[("xo", i)], slot=("xo", i % 4))
        S.flush()


def gla_layer(g, l, xin, xout):
    nc, S = g.nc, g.S
    j = l // 2
    w_in = g.gla_w_in[j]

    with ExitStack() as es:
        uT = sb(es, nc, "a_uT", [128, DC, T], BF16)
        cst = sb(es, nc, "a_cst", [128, 386], F32)
        modv = sb(es, nc, "a_modv", [128, 48], F32)
        wt = [sb(es, nc, "a_wt%d" % i, [128, DC, 512], BF16) for i in range(2)]
        wg = sb(es, nc, "a_wg", [128, DC, 16], BF16)
        glrT = sb(es, nc, "a_glrT", [16, T], F32)
        gw2 = sb(es, nc, "a_gw2", [16, 1024], F32)
        gbias = sb(es, nc, "a_gb", [1, 1024], F32)
        ng = sb(es, nc, "a_ng", [128, EC], F32)
        lbuf = sb(es, nc, "a_l", [128, NT, 1024], F32)
        etmp = [sb(es, nc, "a_et%d" % i, [128, 1024], F32) for i in range(2)]
        E1 = sb(es, nc, "a_E1", [128, NT, 1024], BF16)
        dec = sb(es, nc, "a_dec", [128, 128], F32)
        ttot = sb(es, nc, "a_tt", [128, 8], F32)
        stg = [sb(es, nc, "a_stg%d" % i, [128, 512], BF16) for i in range(8)]
        ztmp = [sb(es, nc, "a_zt%d" % i, [128, 512], F32) for i in range(2)]
        pa = [ps(es, nc, "a_pa%d" % i, [128, 512]) for i in range(4)]
        pg = ps(es, nc, "a_pg", [128, 1024])
        pr = ps(es, nc, "a_pr", [128, 1024])
        ident = cst[:, 0:128]
        triu = cst[:, 128:256]
        ones = cst[:, 256:384]
        cind = cst[:, 384:386]

        S.op("sp", lambda e: e.dma_start(out=cst[:], in_=g.consts), w=["cst"], slot="ld0")
        S.op("sp", lambda e: e.dma_start(out=modv[:], in_=g.modfm[:, l * 48:(l + 1) * 48]), w=["modv"], slot="ld1")
        S.op("sp", lambda e: e.dma_start(out=gw2[:], in_=g.gla_gw2[j]), w=["gw2"], slot="ld2")
        S.op("sp", lambda e: e.dma_start(out=gbias[:], in_=g.gla_gb[j:j + 1, :]), w=["gbias"], slot="ld3")
        S.op("sp", lambda e: e.dma_start(out=ng[:], in_=g.gla_ng[j]), w=["ng"], slot="ld4")
        wgf = sb(es, nc, "a_wgf", [128, DC, 16], F32)
        S.op("sp", lambda e: e.dma_start(
            out=wgf[:], in_=w_in[:, 10240:10256].rearrange("(kc p) n -> p kc n", p=128)), w=["wgf"], slot="wg")
        S.op("dve", lambda e: e.tensor_copy(out=wg[:], in_=wgf[:]), r=["wgf"], w=["wg"])

        groups = []
        for cg in range(2):
            groups.append(("k", 1024 + cg * 512, cg))
        for cg in range(8):
            groups.append(("v", 2048 + cg * 512, cg))
        for cg in range(2):
            groups.append(("q", cg * 512, cg))
        for cg in range(8):
            groups.append(("z", 6144 + cg * 512, cg))

        wl = WLoader(g, es, "a", 2, 8 * 512)

        def load_w(gi):
            kind, c0, _ = groups[gi]
            s = gi % 2
            for q in range(2):
                wl.load(wt[s][:, q * 8:(q + 1) * 8, :],
                        w_in[q * 1024:(q + 1) * 1024, c0:c0 + 512].rearrange("(kc p) n -> p kc n", p=128),
                        8, 512, ("wt", s, q))

        def sub(step):
            if SUB_LIMIT == step:
                S.flush()
                raise _StopBuild()
        sub(0)
        load_w(0)
        load_w(1)
        sub(1)
        emit_build_uT(g, es, l, xin, uT, cst, pa, modv)
        uT_all = [("uT", i, dc) for i in range(NT) for dc in range(DC)]
        sub(2)

        for th in range(2):
            def mm(e, th=th):
                inst = None
                for kc in range(DC):
                    inst = e.matmul(pa[th][0:16, :], wg[:, kc, :], uT[:, kc, th * 512:(th + 1) * 512],
                                    start=(kc == 0), stop=(kc == DC - 1))
                return inst
            S.op("pe", mm, r=uT_all + ["wg"], w=[("pa", th)])
            S.op("dve", lambda e, th=th: e.tensor_copy(out=glrT[:, th * 512:(th + 1) * 512], in_=pa[th][0:16, :]),
                 r=[("pa", th)], w=[("glrT", th)])
        sub(3)
        ptot = pa[3]
        for i in range(NT):
            def mm(e, i=i):
                inst = None
                for hf in range(2):
                    e.matmul(pg[:, hf * 512:(hf + 1) * 512], glrT[:, i * 128:(i + 1) * 128],
                             gw2[:, hf * 512:(hf + 1) * 512], start=True, stop=False)
                    inst = e.matmul(pg[:, hf * 512:(hf + 1) * 512], ones[0:1, :], gbias[0:1, hf * 512:(hf + 1) * 512],
                                    start=False, stop=True)
                return inst
            S.op("pe", mm, r=[("glrT", 0), ("glrT", 1), "gw2", "gbias", "cst"], w=["pg"])
            s = i % 2
            for hf in range(2):
                S.op("act", lambda e, s=s, hf=hf: e.activation(out=etmp[s][:, hf * 512:(hf + 1) * 512],
                                                              in_=pg[:, hf * 512:(hf + 1) * 512], func=AF.Exp, scale=-1.0),
                     r=["pg"], w=[("et", s)])
            S.op("dve", lambda e, s=s: e.tensor_scalar(out=etmp[s][:], in0=etmp[s][:], scalar1=1.0, scalar2=None,
                                                      op0=ALU.add), r=[("et", s)], w=[("et", s)])
            S.op("act", lambda e, s=s, i=i: e.activation(out=lbuf[:, i, :], in_=etmp[s][:], func=AF.Ln),
                 r=[("et", s)], w=[("l", i)])

            def mm2(e, i=i):
                inst = None
                for hf in range(2):
                    inst = e.matmul(pr[:, hf * 512:(hf + 1) * 512], triu, lbuf[:, i, hf * 512:(hf + 1) * 512],
                                    start=True, stop=True)
                return inst
            S.op("pe", mm2, r=[("l", i), "cst"], w=["pr"])
            for hf in range(2):
                S.op("act", lambda e, i=i, hf=hf: e.activation(out=E1[:, i, hf * 512:(hf + 1) * 512],
                                                              in_=pr[:, hf * 512:(hf + 1) * 512], func=AF.Exp,
                                                              scale=-1.0 / TAU), r=["pr"], w=[("E1", i)])

            def mm3(e, i=i):
                inst = None
                for fc in range(8):
                    inst = e.matmul(ptot[:, fc * 16 + 2 * i:fc * 16 + 2 * i + 2], lbuf[:, i, fc * 128:(fc + 1) * 128],
                                    cind, start=True, stop=True)
                return inst
            S.op("pe", mm3, r=[("l", i), "cst"], w=["ptot", ("pa", 3)])
        totsb = sb(es, nc, "a_totsb", [128, 128], F32)
        S.op("dve", lambda e: e.tensor_copy(out=totsb[:], in_=ptot[:, 0:128]), r=["ptot"], w=["totsb"])
        S.op("act", lambda e: e.activation(out=dec[:], in_=totsb[:], func=AF.Exp, scale=-1.0 / TAU),
             r=["totsb"], w=["dec"])
        S.op("dve", lambda e: e.tensor_reduce(out=ttot[:], in_=totsb[:].rearrange("p (f c) -> p f c", c=16),
                                              axis=AX.X, op=ALU.add), r=["totsb"], w=["ttot"])
        S.op("sp", lambda e: e.dma_start(out=g.dec_d, in_=dec[:]), r=["dec"], w=["dec_d"], slot="st0")
        S.op("sp", lambda e: e.dma_start(out=g.tb_d.ap(), in_=ttot[:]), r=["ttot"], w=["tb_d"], slot="st1")

        sub(4)
        kacc = 0
        kst = 0
        kev = 0
        for gi, (kind, c0, cg) in enumerate(groups):
            if gi == 2:
                sub(5)
            if gi == 10:
                sub(6)
            if gi == 12:
                sub(7)
            s = gi % 2
            wkeys = [("wt", s, 0), ("wt", s, 1)]
            if kind in ("k", "v"):
                for i in range(NT):
                    b = kacc % 3
                    kacc += 1
                    sg = kst % 8
                    kst += 1

                    def mm(e, i=i, s=s, b=b):
                        inst = None
                        for kc in range(DC):
                            inst = e.matmul(pa[b][:], uT[:, kc, i * 128:(i + 1) * 128], wt[s][:, kc, :],
                                            start=(kc == 0), stop=(kc == DC - 1))
                        return inst
                    S.op("pe", mm, r=uT_all + wkeys, w=[("pa", b)])
                    if kind == "k":
                        S.op("dve", lambda e, i=i, cg=cg, b=b, sg=sg: e.tensor_tensor(
                            out=stg[sg][:], in0=pa[b][:], in1=E1[:, i, cg * 512:(cg + 1) * 512], op=ALU.mult),
                            r=[("pa", b), ("E1", i)], w=[("stg", sg)])
                        dst = g.kd_d[i, :, cg * 512:(cg + 1) * 512]
                    else:
                        if kev % 2 == 0:
                            S.op("act", lambda e, b=b, sg=sg: e.activation(out=stg[sg][:], in_=pa[b][:], func=AF.Identity),
                                 r=[("pa", b)], w=[("stg", sg)])
                        else:
                            S.op("dve", lambda e, b=b, sg=sg: e.tensor_copy(out=stg[sg][:], in_=pa[b][:]),
                                 r=[("pa", b)], w=[("stg", sg)])
                        kev += 1
                        dst = g.v_d[i, :, cg * 512:(cg + 1) * 512]
                    S.op("sp", lambda e, dst=dst, sg=sg: e.dma_start(out=dst, in_=stg[sg][:]),
                         r=[("stg", sg)], w=[("dr", kind, i, cg)], slot=("stg", sg))
            else:
                for fcl in range(4):
                    fc = cg * 4 + fcl
                    for th in range(2):
                        b = kacc % 3
                        kacc += 1
                        sg = kst % 8
                        kst += 1

                        def mm(e, fcl=fcl, th=th, s=s, b=b):
                            inst = None
                            for kc in range(DC):
                                inst = e.matmul(pa[b][:], wt[s][:, kc, fcl * 128:(fcl + 1) * 128],
                                                uT[:, kc, th * 512:(th + 1) * 512],
                                                start=(kc == 0), stop=(kc == DC - 1))
                            return inst
                        S.op("pe", mm, r=uT_all + wkeys, w=[("pa", b)])
                        if kind == "q":
                            S.op("act", lambda e, b=b, sg=sg: e.activation(out=stg[sg][:], in_=pa[b][:], func=AF.Identity,
                                                                         scale=1.0 / 16.0),
                                 r=[("pa", b)], w=[("stg", sg)])
                            dst = g.qT_d[fc, :, th * 512:(th + 1) * 512]
                        else:
                            zs = kev % 2
                            kev += 1
                            S.op("act", lambda e, b=b, zs=zs: e.activation(out=ztmp[zs][:], in_=pa[b][:], func=AF.Silu),
                                 r=[("pa", b)], w=[("zt", zs)])
                            S.op("dve", lambda e, zs=zs, sg=sg, fc=fc: e.tensor_scalar(
                                out=stg[sg][:], in0=ztmp[zs][:], scalar1=ng[:, fc:fc + 1], scalar2=None, op0=ALU.mult),
                                r=[("zt", zs), "ng"], w=[("stg", sg)])
                            dst = g.zg_d[fc, :, th * 512:(th + 1) * 512]
                        S.op("sp", lambda e, dst=dst, sg=sg: e.dma_start(out=dst, in_=stg[sg][:]),
                             r=[("stg", sg)], w=[("dr", kind, fc, th)], slot=("stg", sg))
            if gi + 2 < len(groups):
                load_w(gi + 2)
        S.flush()

    gla_pass(g, j, with_out=False)
    S.op("pool", lambda e: e.collective_compute(
        "AllGather", ALU.bypass, replica_groups=[list(range(NCORES))],
        ins=[g.Lb_d.ap().opt()], outs=[g.La_d.ap().opt()]), w=["La"], slot=("cc_L", j), inc=1)
    S.op("pool", lambda e: e.collective_compute(
        "AllGather", ALU.bypass, replica_groups=[list(range(NCORES))],
        ins=[g.tb_d.ap().opt()], outs=[g.ta_d.ap().opt()]), w=["ta"], slot=("cc_T", j), inc=1)
    S.flush()
    gla_pass(g, j, with_out=True)
    emit_outproj_ln(g, l, g.gla_w_out[j], g.yT_d, xin, xout)


def gla_pass(g, j, with_out):
    nc, S = g.nc, g.S
    with ExitStack() as es:
        kdh = [sb(es, nc, "p_kd%d" % i, [128, NT, 256], BF16) for i in range(2)]
        vh = [sb(es, nc, "p_v%d" % i, [128, NT, 1024], BF16) for i in range(2)]
        dec = sb(es, nc, "p_dec", [128, 128], F32)
        St = sb(es, nc, "p_S", [128, 2, 1024], F32)
        PS = [ps(es, nc, "p_ps%d" % i, [128, 1024]) for i in range(2)]
        S.op("sp", lambda e: e.dma_start(out=dec[:], in_=g.dec_d), w=["dec"], slot="ld0")
        if with_out:
            cst = sb(es, nc, "p_cst", [128, 386], F32)
            identb = sb(es, nc, "p_idb", [128, 128], BF16)
            qTh = [sb(es, nc, "p_q%d" % i, [128, 2, T], BF16) for i in range(2)]
            zgh = [sb(es, nc, "p_z%d" % i, [128, 8, T], BF16) for i in range(2)]
            Sbf = [sb(es, nc, "p_Sbf%d" % i, [128, 2, 1024], BF16) for i in range(2)]
            Lst = [sb(es, nc, "p_L%d" % i, [128, 2, 1024], F32) for i in range(2)]
            yt = [sb(es, nc, "p_yt%d" % i, [64, 1024], BF16) for i in range(2)]
            junk = sb(es, nc, "p_junk", [64, 1024], BF16)
            yTs = [sb(es, nc, "p_yT%d" % i, [128, 8, T], BF16) for i in range(2)]
            ta = sb(es, nc, "p_ta", [128, NCORES, 8], F32)
            acoef = sb(es, nc, "p_a", [128, NCORES, 8], F32)
            rmask = sb(es, nc, "p_rm", [128, 8], F32)
            ssq = sb(es, nc, "p_ssq", [64, 4], F32)
            rs = sb(es, nc, "p_rs", [64, 2], F32)
            po = ps(es, nc, "p_po", [128, 1024])
            pt = ps(es, nc, "p_pt", [128, 512], BF16)
            S.op("sp", lambda e: e.dma_start(out=cst[:], in_=g.consts), w=["cst"], slot="ld1")
            S.op("dve", lambda e: e.tensor_copy(out=identb[:], in_=cst[:, 0:128]), r=["cst"], w=["idb"])
            S.op("sp", lambda e: e.dma_start(out=rmask[:], in_=g.rmask), w=["rmask"], slot="ld3")
            S.op("sp", lambda e: e.dma_start(out=ta[:], in_=g.ta_d.ap().rearrange("(r p) f -> p r f", p=128)),
                 w=["ta"], slot="ld4")
            S.op("act", lambda e: e.activation(out=acoef[:], in_=ta[:], func=AF.Exp, scale=-1.0 / TAU),
                 r=["ta"], w=["acoef"])
            for jj in range(NCORES):
                S.op("dve", lambda e, jj=jj: e.tensor_scalar(
                    out=acoef[:, jj, :], in0=acoef[:, jj, :], scalar1=-1.0, scalar2=rmask[:, jj:jj + 1],
                    op0=ALU.add, op1=ALU.mult), r=["acoef", "rmask"], w=["acoef"])
            S.op("dve", lambda e: e.tensor_scalar(out=acoef[:], in0=acoef[:], scalar1=1.0, scalar2=None, op0=ALU.add),
                 r=["acoef"], w=["acoef"])
        nl = 0
        for h in range(4):
            s = h % 2
            S.op("sp", lambda e, h=h, s=s: e.dma_start(
                out=kdh[s][:], in_=g.kd_d[:, :, h * 256:(h + 1) * 256].rearrange("i p f -> p i f")),
                w=[("kdh", s)], slot=("kdh", s))
            for q in range(2):
                S.op("sp", lambda e, h=h, s=s, q=q: e.dma_start(
                    out=vh[s][:, q * 4:(q + 1) * 4, :],
                    in_=g.v_d[q * 4:(q + 1) * 4, :, h * 1024:(h + 1) * 1024].rearrange("i p f -> p i f")),
                    w=[("vh", s, q)], slot=("vh", s, q))
            S.op("pool", lambda e: e.memset(St[:], 0.0), w=["St"])
            if with_out:
                S.op("sp", lambda e, h=h, s=s: e.dma_start(
                    out=qTh[s][:], in_=g.qT_d[2 * h:2 * h + 2].rearrange("f p t -> p f t")),
                    w=[("qTh", s)], slot=("qTh", s))
                for q in range(2):
                    S.op("sp", lambda e, h=h, s=s, q=q: e.dma_start(
                        out=zgh[s][:, q * 4:(q + 1) * 4, :],
                        in_=g.zg_d[8 * h + q * 4:8 * h + q * 4 + 4].rearrange("f p t -> p f t")),
                        w=[("zgh", s, q)], slot=("zgh", s, q))
                for jj in range(NCORES - 1):
                    ls = nl % 2
                    nl += 1
                    S.op("sp", lambda e, jj=jj, h=h, ls=ls: e.dma_start(
                        out=Lst[ls][:],
                        in_=g.La_d.ap()[jj * 1024 + 2 * h * 128:jj * 1024 + (2 * h + 2) * 128, :].rearrange(
                            "(b p) n -> p b n", p=128)),
                        r=["La"], w=[("Lst", ls)], slot=("Lst", ls))
                    for kh in range(2):
                        S.op("dve", lambda e, kh=kh, jj=jj, h=h: e.tensor_scalar(
                            out=St[:, kh, :], in0=St[:, kh, :], scalar1=acoef[:, jj, 2 * h + kh:2 * h + kh + 1],
                            scalar2=None, op0=ALU.mult), r=["St", "acoef"], w=["St"])
                        S.op("dve", lambda e, ls=ls, kh=kh, jj=jj: e.scalar_tensor_tensor(
                            out=St[:, kh, :], in0=Lst[ls][:, kh, :], scalar=rmask[:, jj:jj + 1],
                            in1=St[:, kh, :], op0=ALU.mult, op1=ALU.add),
                            r=["St", ("Lst", ls), "rmask"], w=["St"])
            def emit_S(c, h=h, s=s):
                i = c // 2
                p0 = 64 * (c % 2)
                for kh in range(2):
                    def mm(e, s=s, i=i, p0=p0, kh=kh):
                        inst = None
                        for v2 in range(2):
                            inst = e.matmul(PS[kh][:, v2 * 512:(v2 + 1) * 512],
                                            kdh[s][p0:p0 + 64, i, kh * 128:(kh + 1) * 128],
                                            vh[s][p0:p0 + 64, i, v2 * 512:(v2 + 1) * 512], start=True, stop=True)
                        return inst
                    S.op("pe", mm, r=[("kdh", s), ("vh", s, 0), ("vh", s, 1)], w=[("PS", kh)])
                    col = (2 * h + kh) * 16 + c
                    for v2 in range(2):
                        S.op("dve", lambda e, kh=kh, col=col, v2=v2: e.scalar_tensor_tensor(
                            out=St[:, kh, v2 * 512:(v2 + 1) * 512], in0=St[:, kh, v2 * 512:(v2 + 1) * 512],
                            scalar=dec[:, col:col + 1], in1=PS[kh][:, v2 * 512:(v2 + 1) * 512],
                            op0=ALU.mult, op1=ALU.add), r=["St", ("PS", kh), "dec"], w=["St"])
                if with_out:
                    S.op("act", lambda e, c=c: e.activation(out=Sbf[c % 2][:], in_=St[:], func=AF.Identity),
                         r=["St"], w=[("Sbf", c % 2)])
            def emit_O(c, h=h, s=s):

                def mmo(e, s=s, c=c):
                    inst = None
                    for v2 in range(2):
                        for kh in range(2):
                            inst = e.matmul(po[0:64, v2 * 512:(v2 + 1) * 512], qTh[s][:, kh, c * 64:(c + 1) * 64],
                                            Sbf[c % 2][:, kh, v2 * 512:(v2 + 1) * 512], start=(kh == 0), stop=(kh == 1))
                    return inst
                S.op("pe", mmo, r=[("qTh", s), ("Sbf", c % 2)], w=["po"])
                ys = c % 2
                for v2 in range(2):
                    S.op("act", lambda e, ys=ys, v2=v2: e.activation(
                        out=junk[:, v2 * 512:(v2 + 1) * 512], in_=po[0:64, v2 * 512:(v2 + 1) * 512], func=AF.Square,
                        accum_out=ssq[:, 2 * ys + v2:2 * ys + v2 + 1]), r=["po"], w=["junk", ("ssq", ys, v2)])
                S.op("dve", lambda e, ys=ys: e.tensor_tensor(out=rs[:, ys:ys + 1], in0=ssq[:, 2 * ys:2 * ys + 1],
                                                            in1=ssq[:, 2 * ys + 1:2 * ys + 2], op=ALU.add),
                     r=[("ssq", ys, 0), ("ssq", ys, 1)], w=[("rs", ys)])
                S.op("dve", lambda e, ys=ys: e.tensor_scalar(out=rs[:, ys:ys + 1], in0=rs[:, ys:ys + 1],
                                                            scalar1=1.0 / 1024.0, scalar2=RMS_EPS,
                                                            op0=ALU.mult, op1=ALU.add),
                     r=[("rs", ys)], w=[("rs", ys)])
                S.op("act", lambda e, ys=ys: e.activation(out=rs[:, ys:ys + 1], in_=rs[:, ys:ys + 1], func=AF.Sqrt),
                     r=[("rs", ys)], w=[("rs", ys)])
                S.op("dve", lambda e, ys=ys: e.reciprocal(out=rs[:, ys:ys + 1], in_=rs[:, ys:ys + 1]),
                     r=[("rs", ys)], w=[("rs", ys)])
                for v2 in range(2):
                    S.op("dve", lambda e, ys=ys, v2=v2: e.tensor_scalar(
                        out=yt[ys][:, v2 * 512:(v2 + 1) * 512], in0=po[0:64, v2 * 512:(v2 + 1) * 512],
                        scalar1=rs[:, ys:ys + 1], scalar2=None, op0=ALU.mult),
                        r=["po", ("rs", ys)], w=[("yt", ys, v2)])

                def tr(e, ys=ys):
                    inst = None
                    for f in range(8):
                        inst = e.transpose(pt[:, f * 64:(f + 1) * 64], yt[ys][:, f * 128:(f + 1) * 128],
                                           identb[0:64, 0:64])
                    return inst
                S.op("pe", tr, r=[("yt", ys, 0), ("yt", ys, 1), "idb"], w=["pt"])
                S.op("dve", lambda e, s=s, c=c: e.tensor_tensor(
                    out=yTs[s][:, :, c * 64:(c + 1) * 64], in0=pt[:].rearrange("p (f t) -> p f t", t=64),
                    in1=zgh[s][:, :, c * 64:(c + 1) * 64], op=ALU.mult),
                    r=["pt", ("zgh", s, 0), ("zgh", s, 1)], w=[("yTs", s, c)])
            if not with_out:
                for c in range(16):
                    emit_S(c)
            else:
                emit_S(0)
                for c in range(16):
                    if c + 1 < 16:
                        emit_S(c + 1)
                    emit_O(c)
            if with_out:
                S.op("sp", lambda e, h=h, s=s: e.dma_start(
                    out=g.yT_d[8 * h:8 * h + 8].rearrange("f p t -> p f t"), in_=yTs[s][:]),
                    r=[("yTs", s, c) for c in range(16)], w=[("yT_d", h)], slot=("yTs", s))
            else:
                S.op("sp", lambda e, h=h: e.dma_start(
                    out=g.Lb_d.ap()[2 * h * 128:(2 * h + 2) * 128, :].rearrange("(b p) n -> p b n", p=128), in_=St[:]),
                    r=["St"], w=[("Lb", h)], slot="Lst")
        S.flush()


TWO_PI = float(2.0 * np.pi)
PI = float(np.pi)


def _consts2_np():
    k = np.arange(128)
    c = np.zeros((128, 8 + 128 + 128), np.float32)
    h0 = (k < 64).astype(np.float32)
    h1 = 1.0 - h0
    c[:, 0] = h0
    c[:, 1] = h1
    c[:, 2] = -h0
    c[:, 3] = -h1
    c[:, 4] = h1 - h0
    sp = (np.arange(128) // 16)[:, None]
    s = (np.arange(128) // 16)[None, :]
    c[:, 8:136] = (sp <= s).astype(np.float32)
    psw = np.zeros((128, 128), np.float32)
    for m in range(128):
        psw[(m + 64) % 128, m] = 1.0
    c[:, 136:264] = psw
    return c


def s5_layer(g, l, xin, xout):
    j = l // 2
    s5_params(g, j)
    s5_inproj(g, l, j, xin)
    s5_ustack_w(g, j)
    s5_recur(g, j, second=False)
    S = g.S
    S.op("pool", lambda e: e.collective_compute(
        "AllGather", ALU.bypass, replica_groups=[list(range(NCORES))],
        ins=[g.zb_d.ap().opt()], outs=[g.za_d.ap().opt()]), w=["za"], slot=("cc_Z", j), inc=1)
    S.flush()
    s5_recur(g, j, second=True)
    s5_y(g, j)
    s5_glu(g, j)
    emit_outproj_ln(g, l, g.s5_w_out[j], g.yT_d, xin, xout)


def s5_params(g, j):
    nc, S = g.nc, g.S
    with ExitStack() as es:
        c2 = sb(es, nc, "sp_c2", [128, 264], F32)
        cst = sb(es, nc, "sp_cst", [128, 386], F32)
        T_ = {}

        def tl(name):
            if name not in T_:
                T_[name] = sb(es, nc, "sp_" + name, [128, 256], F32)
            return T_[name]

        def TT(o, a, b, op):
            S.op("dve", lambda e: e.tensor_tensor(out=tl(o)[:], in0=tl(a)[:], in1=tl(b)[:], op=op), r=[a, b], w=[o])

        def TS(o, a, s1, op0, s2=None, op1=None):
            if op1 is None:
                S.op("dve", lambda e: e.tensor_scalar(out=tl(o)[:], in0=tl(a)[:], scalar1=s1, scalar2=None, op0=op0),
                     r=[a, "c2"], w=[o])
            else:
                S.op("dve", lambda e: e.tensor_scalar(out=tl(o)[:], in0=tl(a)[:], scalar1=s1, scalar2=s2,
                                                      op0=op0, op1=op1), r=[a, "c2"], w=[o])

        def STT(o, a, sc, b, op0, op1):
            S.op("dve", lambda e: e.scalar_tensor_tensor(out=tl(o)[:], in0=tl(a)[:], scalar=sc, in1=tl(b)[:],
                                                         op0=op0, op1=op1), r=[a, b, "c2"], w=[o])

        def ACTF(o, a, func, scale=1.0):
            S.op("act", lambda e: e.activation(out=tl(o)[:], in_=tl(a)[:], func=func, scale=scale), r=[a], w=[o])

        def cmul(or_, oi_, ar, ai, br, bi):
            TT("cm1", ar, br, ALU.mult)
            TT("cm2", ai, bi, ALU.mult)
            TT("cm3", ar, bi, ALU.mult)
            TT("cm4", ai, br, ALU.mult)
            TT(or_, "cm1", "cm2", ALU.subtract)
            TT(oi_, "cm3", "cm4", ALU.add)

        def sin_of(o, ang, phase):
            TS("sc_r", ang, phase, ALU.add)
            for _ in range(4):
                TS("sc_m", "sc_r", PI, ALU.is_gt, TWO_PI, ALU.mult)
                TT("sc_r", "sc_r", "sc_m", ALU.subtract)
            ACTF(o, "sc_r", AF.Sin)

        S.op("sp", lambda e: e.dma_start(out=c2[:], in_=g.consts2), w=["c2"], slot="ld0")
        S.op("sp", lambda e: e.dma_start(out=cst[:], in_=g.consts), w=["cst"], slot="ld1")
        S.op("sp", lambda e: e.dma_start(out=tl("are")[:], in_=g.s5_are2[j]), w=["are"], slot="ld2")
        S.op("sp", lambda e: e.dma_start(out=tl("aim")[:], in_=g.s5_aim2[j]), w=["aim"], slot="ld3")
        S.op("sp", lambda e: e.dma_start(out=tl("ldt")[:], in_=g.s5_ldt2[j]), w=["ldt"], slot="ld4")
        ACTF("dt", "ldt", AF.Exp)
        TT("ard", "are", "dt", ALU.mult)
        TT("th", "aim", "dt", ALU.mult)
        ACTF("mag", "ard", AF.Exp)
        sin_of("sin", "th", 0.0)
        sin_of("cos", "th", PI / 2.0)
        TT("pr1", "mag", "cos", ALU.mult)
        TT("pi1", "mag", "sin", ALU.mult)
        TT("cm1", "pr1", "pr1", ALU.mult)
        TT("cm2", "pi1", "pi1", ALU.mult)
        TT("im2", "cm1", "cm2", ALU.add)
        S.op("dve", lambda e: e.reciprocal(out=tl("rinv")[:], in_=tl("im2")[:]), r=["im2"], w=["rinv"])
        TT("nr1", "pr1", "rinv", ALU.mult)
        TT("ni1", "pi1", "rinv", ALU.mult)
        TS("ni1", "ni1", -1.0, ALU.mult)
        S.op("dve", lambda e: e.memset(tl("pr0")[:], 1.0), w=["pr0"])
        S.op("dve", lambda e: e.memset(tl("pi0")[:], 0.0), w=["pi0"])
        for e_ in range(2, 16):
            cmul("pr%d" % e_, "pi%d" % e_, "pr%d" % (e_ - 1), "pi%d" % (e_ - 1), "pr1", "pi1")
        for e_ in range(2, 8):
            cmul("nr%d" % e_, "ni%d" % e_, "nr%d" % (e_ - 1), "ni%d" % (e_ - 1), "nr1", "ni1")
        S.op("dve", lambda e: e.tensor_copy(out=tl("xr")[:], in_=tl("pr8")[:]), r=["pr8"], w=["xr"])
        S.op("dve", lambda e: e.tensor_copy(out=tl("xi")[:], in_=tl("pi8")[:]), r=["pi8"], w=["xi"])
        for _ in range(7):
            cmul("xr", "xi", "xr", "xi", "xr", "xi")
        rot = sb(es, nc, "sp_rot", [128, 4, 256], F32)
        S.op("dve", lambda e: e.tensor_copy(out=rot[:, 0, :], in_=tl("pr8")[:]), r=["pr8"], w=["rot0"])
        S.op("dve", lambda e: e.tensor_scalar(out=rot[:, 1, :], in0=tl("pi8")[:], scalar1=c2[:, 4:5], scalar2=None,
                                              op0=ALU.mult), r=["pi8", "c2"], w=["rot1"])
        S.op("dve", lambda e: e.tensor_copy(out=rot[:, 2, :], in_=tl("xr")[:]), r=["xr"], w=["rot2"])
        S.op("dve", lambda e: e.tensor_scalar(out=rot[:, 3, :], in0=tl("xi")[:], scalar1=c2[:, 4:5], scalar2=None,
                                              op0=ALU.mult), r=["xi", "c2"], w=["rot3"])
        S.op("sp", lambda e: e.dma_start(out=g.rot_d, in_=rot[:]), r=["rot0", "rot1", "rot2", "rot3"], w=["rot_d"],
             slot="st0")
        TS("nr", "pr1", -1.0, ALU.add)
        TT("cm1", "are", "are", ALU.mult)
        TT("cm2", "aim", "aim", ALU.mult)
        TT("den", "cm1", "cm2", ALU.add)
        S.op("dve", lambda e: e.reciprocal(out=tl("rden")[:], in_=tl("den")[:]), r=["den"], w=["rden"])
        TT("cm1", "nr", "are", ALU.mult)
        TT("cm2", "pi1", "aim", ALU.mult)
        TT("cr", "cm1", "cm2", ALU.add)
        TT("cr", "cr", "rden", ALU.mult)
        TT("cm1", "pi1", "are", ALU.mult)
        TT("cm2", "nr", "aim", ALU.mult)
        TT("ci", "cm1", "cm2", ALU.subtract)
        TT("ci", "ci", "rden", ALU.mult)
        h0, h1, nh0, nh1, sgn = c2[:, 0:1], c2[:, 1:2], c2[:, 2:3], c2[:, 3:4], c2[:, 4:5]
        TS("t1a", "cr", h0, ALU.mult)
        STT("T1", "ci", h1, "t1a", ALU.mult, ALU.add)
        TS("t1a", "ci", nh0, ALU.mult)
        STT("T2", "cr", h1, "t1a", ALU.mult, ALU.add)
        TS("t1a", "ci", h0, ALU.mult)
        STT("T3", "cr", h1, "t1a", ALU.mult, ALU.add)
        TS("t1a", "cr", h0, ALU.mult)
        STT("T4", "ci", nh1, "t1a", ALU.mult, ALU.add)
        for e_ in range(16):
            TS("vb%d" % e_, "pi%d" % e_, sgn, ALU.mult)
        for e_ in range(8):
            pr_, pi_ = ("pr0", "pi0") if e_ == 0 else ("nr%d" % e_, "ni%d" % e_)
            TS("t1a", pr_, h0, ALU.mult)
            STT("ua%d" % e_, pi_, nh1, "t1a", ALU.mult, ALU.add)
            TS("t1a", pi_, nh0, ALU.mult)
            STT("ub%d" % e_, pr_, nh1, "t1a", ALU.mult, ALU.add)

        NB = 32
        inb = [[sb(es, nc, "sp_in%d_%d" % (q, i), [128, NB, 16], F32) for i in range(4)] for q in range(2)]
        bA = sb(es, nc, "sp_bA", [128, NB, 16], F32)
        bB = sb(es, nc, "sp_bB", [128, NB, 16], F32)
        tmpb = sb(es, nc, "sp_tmpb", [128, NB, 16], F32)
        Bm = sb(es, nc, "sp_Bm", [128, NB, 8, 16], F32)
        Bq = sb(es, nc, "sp_Bq", [128, NB, 8, 16], F32)
        Cm = sb(es, nc, "sp_Cm", [128, NB, 8, 16], F32)
        stg = [sb(es, nc, "sp_stg%d" % i, [128, 4, 128], BF16) for i in range(4)]
        cmb = sb(es, nc, "sp_cmb", [128, NB, 128], BF16)
        pp = [ps(es, nc, "sp_pp%d" % i, [128, 512]) for i in range(4)]
        ident = cst[:, 0:128]
        maskT = c2[:, 8:136]
        srcs = [g.s5_bre2, g.s5_bim2, g.s5_cA, g.s5_cB]

        def bc(name, gs):
            return tl(name)[:, gs].unsqueeze(2).broadcast_to([128, NB, 16])

        def prod(out_ap, X, xk, a, Y, yk, b, gs, wkey):
            S.op("dve", lambda e: e.tensor_tensor(out=out_ap, in0=X, in1=bc(a, gs), op=ALU.mult),
                 r=[xk, a], w=[wkey])
            S.op("dve", lambda e: e.tensor_tensor(out=tmpb[:], in0=Y, in1=bc(b, gs), op=ALU.mult),
                 r=[yk, b], w=["tmpb"])
            S.op("dve", lambda e: e.tensor_tensor(out=out_ap, in0=out_ap, in1=tmpb[:], op=ALU.add),
                 r=[wkey, "tmpb"], w=[wkey])

        kst = 0
        kpp = 0
        for gb in range(256 // NB):
            q = gb % 2
            gs = slice(gb * NB, (gb + 1) * NB)
            for i in range(4):
                S.op("sp", lambda e, q=q, i=i, gs=gs: e.dma_start(out=inb[q][i][:], in_=srcs[i][j][:, gs, :]),
                     w=[("inb", q, i)], slot=("inb", q, i))
            bre, bim, cA, cB = [inb[q][i][:] for i in range(4)]
            kb = [("inb", q, i) for i in range(4)]
            prod(bA[:], bre, kb[0], "T1", bim, kb[1], "T2", gs, "bA")
            prod(bB[:], bre, kb[0], "T3", bim, kb[1], "T4", gs, "bB")
            for s_ in range(8):
                prod(Bm[:, :, s_, :], bA[:], "bA", "pr%d" % (7 - s_), bB[:], "bB", "vb%d" % (7 - s_), gs, ("Bm", s_))
                prod(Bq[:, :, s_, :], bA[:], "bA", "pr%d" % (15 - s_), bB[:], "bB", "vb%d" % (15 - s_), gs, ("Bq", s_))
                prod(Cm[:, :, s_, :], cA, kb[2], "ua%d" % (7 - s_), cB, kb[3], "ub%d" % (7 - s_), gs, ("Cm", s_))
            allBm = [("Bm", s_) for s_ in range(8)]
            allBq = [("Bq", s_) for s_ in range(8)]
            allCm = [("Cm", s_) for s_ in range(8)]
            S.op("act", lambda e: e.activation(out=cmb[:], in_=Cm[:].rearrange("p g s i -> p g (s i)"), func=AF.Identity),
                 r=allCm, w=["cmb"])
            S.op("sp", lambda e, gs=gs: e.dma_start(out=g.CmK_d[:, gs, :], in_=cmb[:]), r=["cmb"], w=[("CmK_d", gb)],
                 slot="cmb")
            for g4 in range(NB // 4):
                b = kpp % 4
                kpp += 1

                def mm(e, g4=g4, b=b):
                    inst = None
                    for gl in range(4):
                        gg = g4 * 4 + gl
                        inst = e.matmul(pp[b][:, gl * 128:(gl + 1) * 128],
                                        Bm[:, gg, :, :].rearrange("p s j -> p (s j)"),
                                        Cm[:, gg, :, :].rearrange("p s i -> p (s i)"), start=True, stop=True)
                    return inst
                S.op("pe", mm, r=allBm + allCm, w=[("pp", b)])
                sg = kst % 4
                kst += 1
                S.op("dve", lambda e, b=b, sg=sg: e.tensor_tensor(
                    out=stg[sg][:], in0=pp[b][:].rearrange("p (g c) -> p g c", c=128),
                    in1=maskT.unsqueeze(1).broadcast_to([128, 4, 128]), op=ALU.mult),
                    r=[("pp", b), "c2"], w=[("stg", sg)])
                g0 = gb * NB + g4 * 4
                S.op("sp", lambda e, sg=sg, g0=g0: e.dma_start(out=g.MT_d[:, g0:g0 + 4, :], in_=stg[sg][:]),
                     r=[("stg", sg)], w=[("MT_d", g0)], slot=("stg", sg))
                b = kpp % 4
                kpp += 1

                def tr(e, g4=g4, b=b):
                    inst = None
                    for gl in range(4):
                        gg = g4 * 4 + gl
                        inst = e.transpose(pp[b][:, gl * 128:(gl + 1) * 128],
                                           Bq[:, gg, :, :].rearrange("p s j -> p (s j)"), ident)
                    return inst
                S.op("pe", tr, r=allBq + ["cst"], w=[("pp", b)])
                sg = kst % 4
                kst += 1
                S.op("act", lambda e, b=b, sg=sg: e.activation(
                    out=stg[sg][:], in_=pp[b][:].rearrange("p (g c) -> p g c", c=128), func=AF.Identity),
                    r=[("pp", b)], w=[("stg", sg)])
                S.op("sp", lambda e, sg=sg, g0=g0: e.dma_start(out=g.BqT_d[:, g0:g0 + 4, :], in_=stg[sg][:]),
                     r=[("stg", sg)], w=[("BqT_d", g0)], slot=("stg", sg))
        S.flush()


def s5_inproj(g, l, j, xin):
    nc, S = g.nc, g.S
    w_in = g.s5_w_in[j]
    with ExitStack() as es:
        uT = sb(es, nc, "i_uT", [128, DC, T], BF16)
        cst = sb(es, nc, "i_cst", [128, 386], F32)
        modv = sb(es, nc, "i_modv", [128, 48], F32)
        wt = [sb(es, nc, "i_wt%d" % i, [128, DC, 512], BF16) for i in range(2)]
        Uk = sb(es, nc, "i_Uk", [128, 256, 8, 16], BF16)
        stg = [sb(es, nc, "i_stg%d" % i, [128, 512], BF16) for i in range(8)]
        pa = [ps(es, nc, "i_pa%d" % i, [128, 512]) for i in range(4)]
        S.op("sp", lambda e: e.dma_start(out=cst[:], in_=g.consts), w=["cst"], slot="ld0")
        S.op("sp", lambda e: e.dma_start(out=modv[:], in_=g.modfm[:, l * 48:(l + 1) * 48]), w=["modv"], slot="ld1")

        wl = WLoader(g, es, "i", 2, 8 * 512)

        def load_w(gi):
            s = gi % 2
            c0 = gi * 512
            for q in range(2):
                wl.load(wt[s][:, q * 8:(q + 1) * 8, :],
                        w_in[q * 1024:(q + 1) * 1024, c0:c0 + 512].rearrange("(kc p) n -> p kc n", p=128),
                        8, 512, ("wt", s, q))
        load_w(0)
        load_w(1)
        emit_build_uT(g, es, l, xin, uT, cst, pa, modv)
        uT_all = [("uT", i, dc) for i in range(NT) for dc in range(DC)]
        kacc = 0
        kst = 0
        kev = 0
        for gi in range(16):
            s = gi % 2
            wkeys = [("wt", s, 0), ("wt", s, 1)]
            if gi < 8:
                for s_ in range(8):
                    b = kacc % 4
                    kacc += 1

                    def mm(e, s_=s_, s=s, b=b):
                        inst = None
                        for kc in range(DC):
                            inst = e.matmul(pa[b][:], uT[:, kc, s_:T:8], wt[s][:, kc, :],
                                            start=(kc == 0), stop=(kc == DC - 1))
                        return inst
                    S.op("pe", mm, r=uT_all + wkeys, w=[("pa", b)])
                    outap = Uk[:, gi * 32:(gi + 1) * 32, s_, :]
                    if kev % 2 == 0:
                        S.op("act", lambda e, b=b, outap=outap: e.activation(
                            out=outap, in_=pa[b][:].rearrange("p (g j) -> p g j", j=16), func=AF.Identity),
                            r=[("pa", b)], w=[("Uk", gi, s_)])
                    else:
                        S.op("dve", lambda e, b=b, outap=outap: e.tensor_copy(
                            out=outap, in_=pa[b][:].rearrange("p (g j) -> p g j", j=16)),
                            r=[("pa", b)], w=[("Uk", gi, s_)])
                    kev += 1
                S.op("sp", lambda e, gi=gi: e.dma_start(
                    out=g.Uk_d[:, gi * 32 * 128:(gi + 1) * 32 * 128],
                    in_=Uk[:, gi * 32:(gi + 1) * 32, :, :].rearrange("p g s j -> p (g s j)")),
                    r=[("Uk", gi, s_) for s_ in range(8)], w=[("Uk_d", gi)], slot=("Uk", gi % 2))
            else:
                cg = gi - 8
                for fcl in range(4):
                    fc = cg * 4 + fcl
                    for th in range(2):
                        b = kacc % 4
                        kacc += 1
                        sg = kst % 8
                        kst += 1

                        def mm(e, fcl=fcl, th=th, s=s, b=b):
                            inst = None
                            for kc in range(DC):
                                inst = e.matmul(pa[b][:], wt[s][:, kc, fcl * 128:(fcl + 1) * 128],
                                                uT[:, kc, th * 512:(th + 1) * 512],
                                                start=(kc == 0), stop=(kc == DC - 1))
                            return inst
                        S.op("pe", mm, r=uT_all + wkeys, w=[("pa", b)])
                        S.op("act", lambda e, b=b, sg=sg: e.activation(out=stg[sg][:], in_=pa[b][:], func=AF.Silu),
                             r=[("pa", b)], w=[("stg", sg)])
                        dst = g.zg_d[fc, :, th * 512:(th + 1) * 512]
                        S.op("sp", lambda e, dst=dst, sg=sg: e.dma_start(out=dst, in_=stg[sg][:]),
                             r=[("stg", sg)], w=[("dr", fc, th)], slot=("stg", sg))
            if gi + 2 < 16:
                load_w(gi + 2)
        S.flush()


def s5_ustack_w(g, j):
    nc, S = g.nc, g.S
    with ExitStack() as es:
        identb = sb(es, nc, "w_idb", [128, 128], BF16)
        identf = sb(es, nc, "w_idf", [128, 128], F32)
        Ukq = [sb(es, nc, "w_Uk%d" % i, [128, 32, 128], BF16) for i in range(2)]
        bq = [sb(es, nc, "w_bq%d" % i, [128, 32, 128], BF16) for i in range(2)]
        ust = [sb(es, nc, "w_ust%d" % i, [128, 8, 128], BF16) for i in range(3)]
        wst = [sb(es, nc, "w_wst%d" % i, [128, 4, 128], F32) for i in range(3)]
        pT = [ps(es, nc, "w_pT%d" % i, [128, 1024], BF16) for i in range(2)]
        pW = [ps(es, nc, "w_pW%d" % i, [128, 512]) for i in range(4)]
        S.op("sp", lambda e: e.dma_start(out=identf[:], in_=g.consts[:, 0:128]), w=["idf"], slot="ld2")
        S.op("dve", lambda e: e.tensor_copy(out=identb[:], in_=identf[:]), r=["idf"], w=["idb"])
        ku = 0
        kw = 0
        for q in range(8):
            s = q % 2
            S.op("sp", lambda e, q=q, s=s: e.dma_start(
                out=Ukq[s][:], in_=g.Uk_d[:, q * 4096:(q + 1) * 4096].rearrange("p (g c) -> p g c", c=128)),
                w=[("Ukq", s)], slot=("Ukq", s))
            S.op("sp", lambda e, q=q, s=s: e.dma_start(out=bq[s][:], in_=g.BqT_d[:, q * 32:(q + 1) * 32, :]),
                 w=[("bq", s)], slot=("bq", s))
            for b8 in range(4):
                pb = ku % 2
                us = ku % 3
                ku += 1

                def tr(e, s=s, b8=b8, pb=pb):
                    inst = None
                    for gl in range(8):
                        inst = e.transpose(pT[pb][:, gl * 128:(gl + 1) * 128], Ukq[s][:, b8 * 8 + gl, :], identb[:])
                    return inst
                S.op("pe", tr, r=[("Ukq", s), "idb"], w=[("pT", pb)])
                S.op("act", lambda e, pb=pb, us=us: e.activation(
                    out=ust[us][:], in_=pT[pb][:].rearrange("p (g c) -> p g c", c=128), func=AF.Identity),
                    r=[("pT", pb)], w=[("ust", us)])
                g0 = q * 32 + b8 * 8
                S.op("sp", lambda e, us=us, g0=g0: e.dma_start(out=g.Ust_d[:, g0:g0 + 8, :], in_=ust[us][:]),
                     r=[("ust", us)], w=[("Ust_d", g0)], slot=("ust", us))
                for h4 in range(2):
                    wb = kw % 4
                    ws = kw % 3
                    kw += 1

                    def mm(e, s=s, b8=b8, h4=h4, us=us, wb=wb):
                        inst = None
                        for gl in range(4):
                            gg = b8 * 8 + h4 * 4 + gl
                            inst = e.matmul(pW[wb][:, gl * 128:(gl + 1) * 128], bq[s][:, gg, :],
                                            ust[us][:, h4 * 4 + gl, :], start=True, stop=True)
                        return inst
                    S.op("pe", mm, r=[("bq", s), ("ust", us)], w=[("pW", wb)])
                    S.op("dve", lambda e, wb=wb, ws=ws: e.tensor_copy(
                        out=wst[ws][:], in_=pW[wb][:].rearrange("p (g c) -> p g c", c=128)),
                        r=[("pW", wb)], w=[("wst", ws)])
                    g1 = g0 + h4 * 4
                    S.op("sp", lambda e, ws=ws, g1=g1: e.dma_start(out=g.W_d[:, g1:g1 + 4, :], in_=wst[ws][:]),
                         r=[("wst", ws)], w=[("W_d", g1)], slot=("wst", ws))
        S.flush()


def s5_recur(g, j, second):
    nc, S = g.nc, g.S
    with ExitStack() as es:
        Wz = sb(es, nc, "r_Wz", [128, 256, 128], F32)
        rot = sb(es, nc, "r_rot", [128, 4, 256], F32)
        c2 = sb(es, nc, "r_c2", [128, 264], F32)
        t1 = sb(es, nc, "r_t1", [128, 256], F32)
        t2 = sb(es, nc, "r_t2", [128, 256], F32)
        zin = sb(es, nc, "r_zin", [128, 256], F32)
        psw = [ps(es, nc, "r_psw%d" % i, [128, 512]) for i in range(2)]
        pswm = c2[:, 136:264]
        S.op("sp", lambda e: e.dma_start(out=c2[:], in_=g.consts2), w=["c2"], slot="ld0")
        S.op("sp", lambda e: e.dma_start(out=rot[:], in_=g.rot_d), w=["rot"], slot="ld1")
        for q in range(8):
            S.op("sp", lambda e, q=q: e.dma_start(out=Wz[:, q * 32:(q + 1) * 32, :], in_=g.W_d[:, q * 32:(q + 1) * 32, :]),
                 w=[("Wq", q)], slot=("Wq", q))
        A1, A2, A1x, A2x = rot[:, 0, :], rot[:, 1, :], rot[:, 2, :], rot[:, 3, :]
        if second:
            za = sb(es, nc, "r_za", [128, NCORES, 256], F32)
            rmask = sb(es, nc, "r_rm", [128, 8], F32)
            ZP = [sb(es, nc, "r_ZP%d" % i, [128, 64, 128], BF16) for i in range(2)]
            S.op("sp", lambda e: e.dma_start(out=za[:], in_=g.za_d.ap().rearrange("(r p) f -> p r f", p=128)),
                 w=["za"], slot="ld2")
            S.op("sp", lambda e: e.dma_start(out=rmask[:], in_=g.rmask), w=["rmask"], slot="ld3")
            S.op("dve", lambda e: e.memset(zin[:], 0.0), w=["zin"])
            for jj in range(NCORES - 1):
                b = jj % 2
                S.op("pe", lambda e, b=b: e.matmul(psw[b][:, 0:256], pswm, zin[:], start=True, stop=True),
                     r=["zin", "c2"], w=[("psw", b)])
                S.op("dve", lambda e: e.tensor_tensor(out=t1[:], in0=A1x, in1=zin[:], op=ALU.mult),
                     r=["zin", "rot"], w=["t1"])
                S.op("dve", lambda e, b=b: e.tensor_tensor(out=t2[:], in0=A2x, in1=psw[b][:, 0:256], op=ALU.mult),
                     r=[("psw", b), "rot"], w=["t2"])
                S.op("dve", lambda e: e.tensor_tensor(out=t1[:], in0=t1[:], in1=t2[:], op=ALU.add),
                     r=["t1", "t2"], w=["t1"])
                S.op("dve", lambda e, jj=jj: e.tensor_tensor(out=t1[:], in0=t1[:], in1=za[:, jj, :], op=ALU.add),
                     r=["t1", "za"], w=["t1"])
                S.op("dve", lambda e: e.tensor_tensor(out=t1[:], in0=t1[:], in1=zin[:], op=ALU.subtract),
                     r=["t1", "zin"], w=["t1"])
                S.op("dve", lambda e, jj=jj: e.scalar_tensor_tensor(
                    out=zin[:], in0=t1[:], scalar=rmask[:, jj:jj + 1], in1=zin[:], op0=ALU.mult, op1=ALU.add),
                    r=["t1", "zin", "rmask"], w=["zin"])
        allW = [("Wq", q) for q in range(8)]
        for n in range(128):
            if n == 0 and not second:
                continue
            b = n % 2
            if n == 0:
                prev = zin[:]
                pk = ["zin"]
            else:
                prev = Wz[:, :, n - 1]
                pk = [("Z", n - 1)] + (allW if n == 1 else [])
            S.op("pe", lambda e, b=b, prev=prev: e.matmul(psw[b][:, 0:256], pswm, prev, start=True, stop=True),
                 r=pk + ["c2"], w=[("psw", b)])
            S.op("dve", lambda e, prev=prev: e.tensor_tensor(out=t1[:], in0=A1, in1=prev, op=ALU.mult),
                 r=pk + ["rot"], w=["t1"])
            S.op("dve", lambda e, b=b: e.tensor_tensor(out=t2[:], in0=A2, in1=psw[b][:, 0:256], op=ALU.mult),
                 r=[("psw", b), "rot"], w=["t2"])
            S.op("dve", lambda e: e.tensor_tensor(out=t1[:], in0=t1[:], in1=t2[:], op=ALU.add),
                 r=["t1", "t2"], w=["t1"])
            S.op("dve", lambda e, n=n: e.tensor_tensor(out=Wz[:, :, n], in0=Wz[:, :, n], in1=t1[:], op=ALU.add),
                 r=["t1"] + allW + ([("Z", n)] if False else []), w=[("Z", n)])
        allZ = [("Z", n) for n in range(1 if not second else 0, 128)] + allW
        if not second:
            S.op("dve", lambda e: e.tensor_copy(out=t2[:], in_=Wz[:, :, 127]), r=allZ + ["t2"], w=["t2"])
            S.op("sp", lambda e: e.dma_start(out=g.zb_d.ap(), in_=t2[:]), r=["t2"], w=["zb"], slot="st0")
        else:
            for q in range(4):
                zs_ = q % 2
                S.op("act", lambda e, q=q, zs_=zs_: e.activation(out=ZP[zs_][:, :, 0], in_=zin[:, q * 64:(q + 1) * 64],
                                                                func=AF.Identity), r=["zin"], w=[("ZPq", zs_, 0)])
                if q % 2 == 0:
                    S.op("act", lambda e, q=q, zs_=zs_: e.activation(out=ZP[zs_][:, :, 1:128],
                                                                    in_=Wz[:, q * 64:(q + 1) * 64, 0:127], func=AF.Identity),
                         r=allZ, w=[("ZPq", zs_, 1)])
                else:
                    S.op("dve", lambda e, q=q, zs_=zs_: e.tensor_copy(out=ZP[zs_][:, :, 1:128],
                                                                     in_=Wz[:, q * 64:(q + 1) * 64, 0:127]),
                         r=allZ, w=[("ZPq", zs_, 1)])
                S.op("sp", lambda e, q=q, zs_=zs_: e.dma_start(out=g.ZP_d[:, q * 64:(q + 1) * 64, :], in_=ZP[zs_][:]),
                     r=[("ZPq", zs_, 0), ("ZPq", zs_, 1)], w=[("ZP_d", q)], slot=("ZP", zs_))
        S.flush()


def s5_y(g, j):
    nc, S = g.nc, g.S
    with ExitStack() as es:
        identb = sb(es, nc, "y_idb", [128, 128], BF16)
        identf = sb(es, nc, "y_idf", [128, 128], F32)
        Yk2 = sb(es, nc, "y_Yk2", [128, 8, E], BF16)
        NB = 16
        ld = [[sb(es, nc, "y_ld%d_%d" % (q, i), [128, NB, 128], BF16) for i in range(5)] for q in range(2)]
        dtab = [sb(es, nc, "y_dt%d" % q, [128, NB, 16], F32) for q in range(2)]
        tmp = [sb(es, nc, "y_tmp%d" % i, [128, 4, 8, 16], F32) for i in range(2)]
        yst = [sb(es, nc, "y_yst%d" % i, [128, T], BF16) for i in range(3)]
        pY = [ps(es, nc, "y_pY%d" % i, [128, 512]) for i in range(4)]
        pT = [ps(es, nc, "y_pT%d" % i, [128, 1024], BF16) for i in range(2)]
        S.op("sp", lambda e: e.dma_start(out=identf[:], in_=g.consts[:, 0:128]), w=["idf"], slot="ld2")
        S.op("dve", lambda e: e.tensor_copy(out=identb[:], in_=identf[:]), r=["idf"], w=["idb"])
        srcs = [g.MT_d, g.CmK_d, g.Ust_d, g.ZP_d]
        ky = 0
        for gb in range(256 // NB):
            q = gb % 2
            gs = slice(gb * NB, (gb + 1) * NB)
            for i in range(4):
                S.op("sp", lambda e, q=q, i=i, gs=gs: e.dma_start(out=ld[q][i][:], in_=srcs[i][:, gs, :]),
                     w=[("ld", q, i)], slot=("ld", q, i))
            S.op("sp", lambda e, q=q, gb=gb: e.dma_start(
                out=ld[q][4][:], in_=g.Uk_d[:, gb * NB * 128:(gb + 1) * NB * 128].rearrange("p (g c) -> p g c", c=128)),
                w=[("ld", q, 4)], slot=("ld", q, 4))
            S.op("sp", lambda e, q=q, gs=gs: e.dma_start(out=dtab[q][:], in_=g.s5_dtab[j][:, gs, :]),
                 w=[("dtab", q)], slot=("dtab", q))
            MT, CmK, Ust, ZP, Uk = [ld[q][i] for i in range(5)]
            for g4 in range(NB // 4):
                b = ky % 4
                ts = ky % 2
                ky += 1

                def mm(e, g4=g4, b=b, MT=MT, CmK=CmK, Ust=Ust, ZP=ZP):
                    inst = None
                    for gl in range(4):
                        gg = g4 * 4 + gl
                        e.matmul(pY[b][:, gl * 128:(gl + 1) * 128], Ust[:, gg, :], MT[:, gg, :], start=True, stop=False)
                        inst = e.matmul(pY[b][:, gl * 128:(gl + 1) * 128], ZP[:, gg, :], CmK[:, gg, :],
                                        start=False, stop=True)
                    return inst
                S.op("pe", mm, r=[("ld", q, i) for i in range(4)], w=[("pY", b)])
                g0 = gb * NB + g4 * 4
                S.op("dve", lambda e, q=q, g4=g4, ts=ts, Uk=Uk: e.tensor_tensor(
                    out=tmp[ts][:], in0=Uk[:, g4 * 4:(g4 + 1) * 4, :].rearrange("p g (s j) -> p g s j", j=16),
                    in1=dtab[q][:, g4 * 4:(g4 + 1) * 4, :].unsqueeze(2).broadcast_to([128, 4, 8, 16]), op=ALU.mult),
                    r=[("ld", q, 4), ("dtab", q)], w=[("tmp", ts)])
                S.op("dve", lambda e, b=b, ts=ts: e.tensor_tensor(
                    out=tmp[ts][:], in0=tmp[ts][:], in1=pY[b][:].rearrange("p (g s i) -> p g s i", s=8, i=16),
                    op=ALU.add), r=[("tmp", ts), ("pY", b)], w=[("tmp", ts)])
                S.op("act", lambda e, ts=ts, g0=g0: e.activation(
                    out=Yk2[:, :, g0 * 16:(g0 + 4) * 16].rearrange("p s (g i) -> p g s i", i=16),
                    in_=tmp[ts][:], func=AF.Gelu_apprx_tanh), r=[("tmp", ts)], w=[("Yk2", g0)])
        allY = [("Yk2", g0) for g0 in range(0, 256, 4)]
        kt = 0
        for c in range(EC):
            pb = kt % 2
            ys = kt % 3
            kt += 1

            def tr(e, c=c, pb=pb):
                inst = None
                for s_ in range(8):
                    inst = e.transpose(pT[pb][:, s_ * 128:(s_ + 1) * 128], Yk2[:, s_, c * 128:(c + 1) * 128], identb[:])
                return inst
            S.op("pe", tr, r=allY + ["idb"], w=[("pT", pb)])
            if c % 2 == 0:
                S.op("act", lambda e, pb=pb, ys=ys: e.activation(
                    out=yst[ys][:].rearrange("p (n s) -> p s n", s=8),
                    in_=pT[pb][:].rearrange("p (s n) -> p s n", n=128), func=AF.Identity),
                    r=[("pT", pb)], w=[("yst", ys)])
            else:
                S.op("dve", lambda e, pb=pb, ys=ys: e.tensor_copy(
                    out=yst[ys][:].rearrange("p (n s) -> p s n", s=8),
                    in_=pT[pb][:].rearrange("p (s n) -> p s n", n=128)),
                    r=[("pT", pb)], w=[("yst", ys)])
            S.op("sp", lambda e, c=c, ys=ys: e.dma_start(out=g.ygT_d[c], in_=yst[ys][:]),
                 r=[("yst", ys)], w=[("ygT_d", c)], slot=("yst", ys))
        S.flush()


def s5_glu(g, j):
    nc, S = g.nc, g.S
    w_glu = g.s5_w_glu[j]
    with ExitStack() as es:
        yT = sb(es, nc, "g_yT", [128, EC, T], BF16)
        wg = [sb(es, nc, "g_wg%d" % i, [128, EC, 256], BF16) for i in range(2)]
        zs = [sb(es, nc, "g_zs%d" % i, [128, 2, T], BF16) for i in range(2)]
        bgl = sb(es, nc, "g_bgl", [128, EC], F32)
        sig = [sb(es, nc, "g_sig%d" % i, [128, 512], F32) for i in range(2)]
        stg = [sb(es, nc, "g_stg%d" % i, [128, 512], BF16) for i in range(4)]
        pa = [ps(es, nc, "g_pa%d" % i, [128, 512]) for i in range(4)]
        S.op("sp", lambda e: e.dma_start(out=bgl[:], in_=g.s5_bglu[j]), w=["bgl"], slot="ld0")
        for q in range(4):
            S.op("sp", lambda e, q=q: e.dma_start(
                out=yT[:, q * 8:(q + 1) * 8, :], in_=g.ygT_d[q * 8:(q + 1) * 8].rearrange("f p t -> p f t")),
                w=[("yT", q)], slot=("yT", q))
        allyT = [("yT", q) for q in range(4)]

        wl = WLoader(g, es, "g", 2, 16 * 256)

        def load_w(cg):
            s = cg % 2
            for q in range(2):
                wl.load(wg[s][:, q * 16:(q + 1) * 16, :],
                        w_glu[q * 2048:(q + 1) * 2048, cg * 256:(cg + 1) * 256].rearrange("(kc p) n -> p kc n", p=128),
                        16, 256, ("wg", s, q))
        load_w(0)
        load_w(1)
        k = 0
        for cg in range(16):
            s = cg % 2
            S.op("sp", lambda e, cg=cg, s=s: e.dma_start(
                out=zs[s][:], in_=g.zg_d[cg * 2:cg * 2 + 2].rearrange("f p t -> p f t")),
                w=[("zs", s)], slot=("zs", s))
            for fcl in range(2):
                fc = cg * 2 + fcl
                for th in range(2):
                    b = k % 4
                    sg = k % 4
                    ss = k % 2
                    k += 1

                    def mm(e, s=s, fcl=fcl, th=th, b=b):
                        inst = None
                        for kc in range(EC):
                            inst = e.matmul(pa[b][:], wg[s][:, kc, fcl * 128:(fcl + 1) * 128],
                                            yT[:, kc, th * 512:(th + 1) * 512], start=(kc == 0), stop=(kc == EC - 1))
                        return inst
                    S.op("pe", mm, r=allyT + [("wg", s, 0), ("wg", s, 1)], w=[("pa", b)])
                    S.op("dve", lambda e, b=b, ss=ss, fc=fc: e.tensor_scalar(
                        out=sig[ss][:], in0=pa[b][:], scalar1=bgl[:, fc:fc + 1], scalar2=None, op0=ALU.add),
                        r=[("pa", b), "bgl"], w=[("sig", ss)])
                    S.op("act", lambda e, ss=ss: e.activation(out=sig[ss][:], in_=sig[ss][:], func=AF.Sigmoid),
                         r=[("sig", ss)], w=[("sig", ss)])
                    S.op("dve", lambda e, ss=ss, fc=fc, th=th: e.tensor_tensor(
                        out=sig[ss][:], in0=sig[ss][:], in1=yT[:, fc, th * 512:(th + 1) * 512], op=ALU.mult),
                        r=[("sig", ss)] + allyT, w=[("sig", ss)])
                    S.op("dve", lambda e, ss=ss, sg=sg, s=s, fcl=fcl, th=th: e.tensor_tensor(
                        out=stg[sg][:], in0=sig[ss][:], in1=zs[s][:, fcl, th * 512:(th + 1) * 512], op=ALU.mult),
                        r=[("sig", ss), ("zs", s)], w=[("stg", sg)])
                    dst = g.yT_d[fc, :, th * 512:(th + 1) * 512]
                    S.op("sp", lambda e, dst=dst, sg=sg: e.dma_start(out=dst, in_=stg[sg][:]),
                         r=[("stg", sg)], w=[("yT_d", fc, th)], slot=("stg", sg))
            if cg + 2 < 16:
                load_w(cg + 2)
        S.flush()


_PROG = {}


def _get_prog(n_layers):
    if n_layers not in _PROG:
        _PROG[n_layers] = build_program(n_layers)
    return _PROG[n_layers]


def make_in_maps(inp):
    f = np.float32
    x = np.ascontiguousarray(inp["x"], dtype=f).reshape(NCORES * T, D)
    c = np.asarray(inp["c"], dtype=f).reshape(D)
    cT = np.ascontiguousarray(c.reshape(DC, 128).T)
    consts = _consts_np()
    ng = np.ascontiguousarray(np.asarray(inp["gla_norm_g"], f).reshape(2, EC, 128).transpose(0, 2, 1))
    consts2 = _consts2_np()

    def rep2(a):
        return np.ascontiguousarray(np.concatenate([a, a], axis=1))
    are2 = rep2(np.asarray(inp["s5_a_re"], f).transpose(0, 2, 1))
    aim2 = rep2(np.asarray(inp["s5_a_im"], f).transpose(0, 2, 1))
    ldt2 = np.ascontiguousarray(np.broadcast_to(np.asarray(inp["s5_log_dt"], f)[:, None, :], (2, 128, 256)))
    bre2 = rep2(np.asarray(inp["s5_b_re"], f).transpose(0, 2, 1, 3))
    bim2 = rep2(np.asarray(inp["s5_b_im"], f).transpose(0, 2, 1, 3))
    cA = rep2(np.asarray(inp["s5_c_re"], f).transpose(0, 3, 1, 2))
    cB = rep2(np.asarray(inp["s5_c_im"], f).transpose(0, 3, 1, 2))
    dtab = np.ascontiguousarray(np.broadcast_to(np.asarray(inp["s5_d"], f).reshape(2, 1, 256, 16), (2, 128, 256, 16)))
    bglu = np.ascontiguousarray(np.asarray(inp["s5_b_glu"], f).reshape(2, EC, 128).transpose(0, 2, 1))
    s5 = {"consts2": consts2, "s5_w_in": np.asarray(inp["s5_w_in"], f), "s5_w_glu": np.asarray(inp["s5_w_glu"], f),
          "s5_w_out": np.asarray(inp["s5_w_out"], f), "s5_are2": are2, "s5_aim2": aim2, "s5_ldt2": ldt2,
          "s5_bre2": bre2, "s5_bim2": bim2, "s5_cA": cA, "s5_cB": cB, "s5_dtab": dtab, "s5_bglu": bglu}
    maps = []
    for r in range(NCORES):
        adaw = np.ascontiguousarray(np.asarray(inp["ada_w"], f)[:, :, r * 768:(r + 1) * 768])
        adab = np.asarray(inp["ada_b"], f)[:, r * 768:(r + 1) * 768].reshape(DEPTH, 6, 128)
        adab = np.ascontiguousarray(adab.transpose(2, 0, 1).reshape(128, DEPTH * 6))
        rmask = np.zeros((128, 8), f)
        rmask[:, :r] = 1.0
        m = {
            "x_in": np.ascontiguousarray(x[r * T:(r + 1) * T]),
            "cT": cT, "adaw": adaw, "adab": adab, "consts": consts, "rmask": rmask,
            "ln_g": np.asarray(inp["ln_g"], f), "ln_b": np.asarray(inp["ln_b"], f),
            "gla_w_in": np.asarray(inp["gla_w_in"], f), "gla_gw2": np.asarray(inp["gla_gate_w2"], f),
            "gla_gb": np.asarray(inp["gla_gate_b"], f), "gla_ng": ng,
            "gla_w_out": np.asarray(inp["gla_w_out"], f),
        }
        if N_LAYERS >= 2:
            m.update(s5)
        maps.append(m)
    return maps


def kernel(**inputs):
    nc = _get_prog(N_LAYERS)
    in_maps = make_in_maps(inputs)
    res = run_bass_kernel_spmd(nc, in_maps, core_ids=list(range(NCORES)))
    out = np.concatenate([np.asarray(r["y_out"], dtype=np.float32) for r in res.results], axis=0)
    return out.reshape(1, NCORES * T, D)
```
